# Optimizing a Trainium2 kernel written in Bass

```python
import math
import jax, jax.numpy as jnp
from jax import lax
import numpy as np

D_MODEL = 1024
BATCH = 16
SEQ = 2048
DEPTH = 1

CHUNK = 64
Q_BLOCK = 128
FOX_HEADS = 8
FOX_HEAD_DIM = 64
FOX_WIDTH = FOX_HEADS * FOX_HEAD_DIM
MLSTM_HEADS = 4
MLSTM_INNER = D_MODEL
MLSTM_V_DIM = MLSTM_INNER // MLSTM_HEADS
MLSTM_QK_DIM = MLSTM_V_DIM // 2
CONV_WIDTH = 4
D_FF = -(-(8 * D_MODEL) // (3 * 256)) * 256
FOX_FORGET_BIAS = 3.0
MLSTM_FORGET_BIAS = 3.0
LN_EPS = 1e-5
IN_SPLITS = (FOX_WIDTH, FOX_WIDTH, FOX_WIDTH, FOX_HEADS,
             MLSTM_INNER, MLSTM_INNER, MLSTM_HEADS, MLSTM_HEADS, MLSTM_INNER,
             D_MODEL, D_MODEL)
D_IN = sum(IN_SPLITS)

kernel_name = "hybrid_fox_mlstm_adaln_deepnorm_block"


def _ln(x, g=None, b=None):
    xf = x.astype(jnp.float32)
    mu = jnp.mean(xf, axis=-1, keepdims=True)
    var = jnp.mean(jnp.square(xf - mu), axis=-1, keepdims=True)
    y = ((xf - mu) * lax.rsqrt(var + LN_EPS)).astype(x.dtype)
    if g is not None:
        y = y * g + b
    return y


def _forgetting_attention(q, k, v, f_logit):
    B, S, H, d = q.shape
    log_f = jax.nn.log_sigmoid(f_logit.astype(jnp.float32))
    F = jnp.cumsum(log_f, axis=1).transpose(0, 2, 1)
    q = q.transpose(0, 2, 1, 3)
    k = k.transpose(0, 2, 1, 3)
    v = v.transpose(0, 2, 1, 3)
    scale = d ** -0.5
    outs = []
    for start in range(0, S, Q_BLOCK):
        end = start + Q_BLOCK
        logits = jnp.einsum('bhqd,bhkd->bhqk', q[:, :, start:end], k[:, :, :end]).astype(jnp.float32) * scale
        logits = logits + F[:, :, start:end, None] - F[:, :, None, :end]
        q_pos = jnp.arange(start, end)[:, None]
        k_pos = jnp.arange(end)[None, :]
        logits = jnp.where(k_pos <= q_pos, logits, -jnp.inf)
        p = jax.nn.softmax(logits, axis=-1).astype(v.dtype)
        outs.append(jnp.einsum('bhqk,bhkd->bhqd', p, v[:, :, :end]))
    o = jnp.concatenate(outs, axis=2)
    return o.transpose(0, 2, 1, 3).reshape(B, S, H * d)


def _mlstm_chunkwise(q, k, v, i_pre, f_pre):
    out_dtype = v.dtype
    B, S, H, dk = q.shape
    dv = v.shape[-1]
    NC, L = S // CHUNK, CHUNK
    f32 = jnp.float32
    q = (q.astype(f32) * dk ** -0.5).reshape(B, NC, L, H, dk).transpose(0, 3, 1, 2, 4)
    k = k.astype(f32).reshape(B, NC, L, H, dk).transpose(0, 3, 1, 2, 4)
    v = v.astype(f32).reshape(B, NC, L, H, dv).transpose(0, 3, 1, 2, 4)
    ig = i_pre.astype(f32).reshape(B, NC, L, H).transpose(0, 3, 1, 2)
    lf = jax.nn.log_sigmoid(f_pre.astype(f32)).reshape(B, NC, L, H).transpose(0, 3, 1, 2)
    b = jnp.cumsum(lf, axis=-1)
    g = b[..., -1]
    a = g[..., None] - b + ig

    def step(carry, xs):
        C, n, m = carry
        k_c, v_c, a_c, g_c = xs
        m_new = jnp.maximum(g_c + m, jnp.max(a_c, axis=-1))
        decay = jnp.exp(g_c + m - m_new)
        w = jnp.exp(a_c - m_new[..., None])
        C_new = decay[..., None, None] * C + jnp.einsum('bhl,bhlk,bhlv->bhkv', w, k_c, v_c)
        n_new = decay[..., None] * n + jnp.einsum('bhl,bhlk->bhk', w, k_c)
        return (C_new, n_new, m_new), (C, n, m)

    init = (jnp.zeros((B, H, dk, dv), f32), jnp.zeros((B, H, dk), f32), jnp.zeros((B, H), f32))
    xs = (k.transpose(2, 0, 1, 3, 4), v.transpose(2, 0, 1, 3, 4),
          a.transpose(2, 0, 1, 3), g.transpose(2, 0, 1))
    _, (C_prev, n_prev, m_prev) = lax.scan(step, init, xs)
    C_prev = C_prev.transpose(1, 2, 0, 3, 4)
    n_prev = n_prev.transpose(1, 2, 0, 3)
    m_prev = m_prev.transpose(1, 2, 0)

    causal = jnp.tril(jnp.ones((L, L), dtype=bool))
    D = jnp.where(causal, b[..., :, None] - b[..., None, :] + ig[..., None, :], -jnp.inf)
    inter = b + m_prev[..., None]
    m_t = jnp.maximum(inter, jnp.max(D, axis=-1))
    scores = jnp.einsum('bhcld,bhcsd->bhcls', q, k) * jnp.exp(D - m_t[..., None])
    w_inter = jnp.exp(inter - m_t)
    num = (w_inter[..., None] * jnp.einsum('bhcld,bhcdv->bhclv', q, C_prev)
           + jnp.einsum('bhcls,bhcsv->bhclv', scores, v))
    den = w_inter * jnp.einsum('bhcld,bhcd->bhcl', q, n_prev) + jnp.sum(scores, axis=-1)
    h = num / jnp.maximum(jnp.abs(den), jnp.exp(-m_t))[..., None]
    return h.transpose(0, 2, 3, 1, 4).reshape(B, S, H * dv).astype(out_dtype)


def _causal_depthwise_conv(u, w, b):
    out = lax.conv_general_dilated(u, w[:, None, :], window_strides=(1,),
                                   padding=[(CONV_WIDTH - 1, 0)],
                                   dimension_numbers=('NWC', 'WIO', 'NWC'),
                                   feature_group_count=u.shape[-1])
    return out + b


def _head_norm(h, g, n_heads):
    B, S, W = h.shape
    return _ln(h.reshape(B, S, n_heads, W // n_heads)).reshape(B, S, W) * g


def _hybrid_mixer(h, w_in, b_in, conv_w, conv_b, w_mq, w_mk, mh_norm_g, w_pa, w_pb, w_out):
    B, S, _ = h.shape
    proj = jnp.einsum('bsd,de->bse', h, w_in) + b_in
    fq, fk, fv, ff, mu, mv, mi, mf, mo, ga, gb = jnp.split(
        proj, np.cumsum(IN_SPLITS)[:-1].tolist(), axis=-1)
    shp = (B, S, FOX_HEADS, FOX_HEAD_DIM)
    att = _forgetting_attention(fq.reshape(shp), fk.reshape(shp), fv.reshape(shp), ff)
    u = jax.nn.silu(_causal_depthwise_conv(mu, conv_w, conv_b))
    uh = u.reshape(B, S, MLSTM_HEADS, MLSTM_V_DIM)
    mq = jnp.einsum('bshi,hik->bshk', uh, w_mq)
    mk = jnp.einsum('bshi,hik->bshk', uh, w_mk)
    hm = _mlstm_chunkwise(mq, mk, mv.reshape(B, S, MLSTM_HEADS, MLSTM_V_DIM), mi, mf)
    hm = _head_norm(hm, mh_norm_g, MLSTM_HEADS) * jax.nn.sigmoid(mo)
    y = (jax.nn.sigmoid(ga) * jnp.einsum('bsi,id->bsd', att, w_pa)
         + jax.nn.sigmoid(gb) * jnp.einsum('bsi,id->bsd', hm, w_pb))
    return jnp.einsum('bsd,de->bse', y, w_out)


def _swiglu(h, w_ffn_in, w_ffn_down):
    gate, up = jnp.split(jnp.einsum('bsd,df->bsf', h, w_ffn_in), 2, axis=-1)
    return jnp.einsum('bsf,fd->bsd', jax.nn.silu(gate) * up, w_ffn_down)


def setup_inputs(seed: int = 0) -> dict:
    key = jax.random.key(seed)
    ks = jax.random.split(key, 24)
    f32 = jnp.float32
    beta = (8.0 * DEPTH) ** -0.25
    nrm = lambda k, shape, s: jax.random.normal(k, shape, f32) * s
    ada_offset = jnp.concatenate([jnp.zeros((2 * D_MODEL,), f32), jnp.ones((D_MODEL,), f32),
                                  jnp.zeros((2 * D_MODEL,), f32), jnp.ones((D_MODEL,), f32)])
    parts = [jnp.zeros((n,), f32) for n in IN_SPLITS]
    parts[3] = jnp.full((FOX_HEADS,), FOX_FORGET_BIAS, f32)
    parts[7] = jnp.full((MLSTM_HEADS,), MLSTM_FORGET_BIAS, f32)
    in_offset = jnp.concatenate(parts)
    return {
        "x": nrm(ks[0], (BATCH, SEQ, D_MODEL), 1.0),
        "c": nrm(ks[1], (BATCH, D_MODEL), 1.0),
        "w_ada": nrm(ks[2], (DEPTH, D_MODEL, 6 * D_MODEL), 0.3 * D_MODEL ** -0.5),
        "b_ada": nrm(ks[3], (DEPTH, 6 * D_MODEL), 0.02) + ada_offset,
        "w_in": nrm(ks[4], (DEPTH, D_MODEL, D_IN), D_MODEL ** -0.5),
        "b_in": nrm(ks[5], (DEPTH, D_IN), 0.02) + in_offset,
        "conv_w": nrm(ks[6], (DEPTH, CONV_WIDTH, MLSTM_INNER), CONV_WIDTH ** -0.5),
        "conv_b": nrm(ks[7], (DEPTH, MLSTM_INNER), 0.02),
        "w_mq": nrm(ks[8], (DEPTH, MLSTM_HEADS, MLSTM_V_DIM, MLSTM_QK_DIM), MLSTM_V_DIM ** -0.5),
        "w_mk": nrm(ks[9], (DEPTH, MLSTM_HEADS, MLSTM_V_DIM, MLSTM_QK_DIM), MLSTM_V_DIM ** -0.5),
        "mh_norm_g": 1.0 + nrm(ks[10], (DEPTH, MLSTM_INNER), 0.02),
        "w_pa": nrm(ks[11], (DEPTH, FOX_WIDTH, D_MODEL), FOX_WIDTH ** -0.5),
        "w_pb": nrm(ks[12], (DEPTH, MLSTM_INNER, D_MODEL), MLSTM_INNER ** -0.5),
        "w_out": nrm(ks[13], (DEPTH, D_MODEL, D_MODEL), beta * D_MODEL ** -0.5),
        "ln1_g": 1.0 + nrm(ks[14], (DEPTH, D_MODEL), 0.02),
        "ln1_b": nrm(ks[15], (DEPTH, D_MODEL), 0.02),
        "w_ffn_in": nrm(ks[16], (DEPTH, D_MODEL, 2 * D_FF), D_MODEL ** -0.5),
        "w_ffn_down": nrm(ks[17], (DEPTH, D_FF, D_MODEL), beta * D_FF ** -0.5),
        "ln2_g": 1.0 + nrm(ks[18], (DEPTH, D_MODEL), 0.02),
        "ln2_b": nrm(ks[19], (DEPTH, D_MODEL), 0.02),
    }


def reference(x, c, w_ada, b_ada, w_in, b_in, conv_w, conv_b, w_mq, w_mk, mh_norm_g,
              w_pa, w_pb, w_out, ln1_g, ln1_b, w_ffn_in, w_ffn_down, ln2_g, ln2_b):
    alpha = (2.0 * DEPTH) ** 0.25
    for l in range(DEPTH):
        mod = jnp.einsum('bd,de->be', jax.nn.silu(c), w_ada[l]) + b_ada[l]
        sh1, sc1, g1, sh2, sc2, g2 = [m[:, None, :] for m in jnp.split(mod, 6, axis=-1)]
        h = _ln(x) * (1.0 + sc1) + sh1
        y = _hybrid_mixer(h, w_in[l], b_in[l], conv_w[l], conv_b[l], w_mq[l], w_mk[l],
                          mh_norm_g[l], w_pa[l], w_pb[l], w_out[l])
        x = _ln(alpha * x + g1 * y, ln1_g[l], ln1_b[l])
        h = _ln(x) * (1.0 + sc2) + sh2
        x = _ln(alpha * x + g2 * _swiglu(h, w_ffn_in[l], w_ffn_down[l]), ln2_g[l], ln2_b[l])
    return x
```

```python
import numpy as np
from contextlib import ExitStack
import concourse.bass as bass
import concourse.mybir as mybir
from concourse.bass_utils import run_bass_kernel_spmd

F32 = mybir.dt.float32
BF16 = mybir.dt.bfloat16
AF = mybir.ActivationFunctionType
ALU = mybir.AluOpType

P = 128
S = 2048
D = 1024
NB = 2
NBLK = S // P
NTC = 4
DFF = 2816
NFC = DFF // P
FG_SIZES = [4, 4, 4, 4, 4, 2]
ALPHA = 2.0 ** 0.25
EPS = 1e-5
KD = 4

DEBUG_STAGE = None
_DUMPS = []


class Reg:
    __slots__ = ("name", "w", "r")

    def __init__(self, name):
        self.name = name
        self.w = {}
        self.r = {}


class Eng:
    def __init__(self, kb, name, h):
        self.name = name
        self.h = h
        self.semname = "s_" + name
        self.sem = kb.new_sem(self.semname)
        self.seq = 0
        self.known = {}
        self.dma_i = 0
        self.dsem = []


class KB:
    def __init__(self, nc, es):
        self.nc = nc
        self.es = es
        self.sems = {}
        self.pe = Eng(self, "pe", nc.tensor)
        self.act = Eng(self, "act", nc.scalar)
        self.dve = Eng(self, "dve", nc.vector)
        self.pool = Eng(self, "pool", nc.gpsimd)
        self.sp = Eng(self, "sp", nc.sync)
        self.engs = [self.pe, self.act, self.dve, self.pool, self.sp]
        for q in (self.sp, self.pool, self.act):
            for i in range(KD):
                nm = "d_%s%d" % (q.name, i)
                self.new_sem(nm)
                q.dsem.append(nm)
        self.out_events = {}
        self.n_ins = 0
        self.pe_n = 0
        self.marks = []

    def new_sem(self, name):
        s = self.es.enter_context(self.nc.semaphore(name))
        self.sems[name] = s
        return s

    def _deps(self, eng, r, w, waw=True):
        deps = {}
        for x in r:
            for k, v in x.w.items():
                if deps.get(k, 0) < v:
                    deps[k] = v
        for x in w:
            if waw:
                for k, v in x.w.items():
                    if deps.get(k, 0) < v:
                        deps[k] = v
            for k, v in x.r.items():
                if deps.get(k, 0) < v:
                    deps[k] = v
        for k, v in deps.items():
            if eng is self.pe and k == self.pe.semname:
                continue
            if eng.known.get(k, 0) < v:
                eng.h.wait_ge(self.sems[k], v)
                eng.known[k] = v
                self.n_ins += 1

    def op(self, eng, fn, r=(), w=(), signal=True, waw=True):
        self._deps(eng, r, w, waw)
        ins = fn()
        self.n_ins += 1
        if eng is self.pe:
            self.pe_n += 1
        if signal:
            eng.seq += 1
            ins.then_inc(eng.sem, 1)
            ev = eng.seq
        else:
            ev = eng.seq + 1
        k = eng.semname
        for x in w:
            if x.w.get(k, 0) < ev:
                x.w[k] = ev
        for x in r:
            if x.r.get(k, 0) < ev:
                x.r[k] = ev
        return ins

    def dma(self, q, out, in_, r=(), w=(), is_out=False, **kw):
        i = q.dma_i
        nm = q.dsem[i % KD]
        tgt = 16 * (i // KD + 1)
        prev = tgt - 16
        if prev > 0 and q.known.get(nm, 0) < prev:
            q.h.wait_ge(self.sems[nm], prev)
            q.known[nm] = prev
        self._deps(q, r, w)
        q.h.dma_start(out=out, in_=in_, **kw).then_inc(self.sems[nm], 16)
        self.n_ins += 1
        q.dma_i += 1
        for x in w:
            x.w[nm] = tgt
        for x in r:
            x.r[nm] = tgt
        if is_out:
            self.out_events[nm] = tgt

    def mark(self, label):
        self.marks.append((label, self.pe_n))

    def barrier(self):
        ev = {}
        for e in self.engs:
            if e.seq > 0:
                ev[e.semname] = e.seq
            for j, nm in enumerate(e.dsem):
                n = (e.dma_i - j + KD - 1) // KD if e.dma_i > j else 0
                if n > 0:
                    ev[nm] = 16 * n
        for e in self.engs:
            for k, v in ev.items():
                if e.known.get(k, 0) < v:
                    e.h.wait_ge(self.sems[k], v)
                    e.known[k] = v
                    self.n_ins += 1

    def mm(self, out, lhsT, rhs, start, stop, r=(), w=(), signal=None, sgc=False):
        if signal is None:
            signal = stop
        return self.op(self.pe, lambda: self.nc.tensor.matmul(out, lhsT, rhs, start=start, stop=stop,
                                                              skip_group_check=sgc),
                       r=r, w=w, signal=signal)

    def tr(self, out, in_, ident, r=(), w=(), signal=True):
        return self.op(self.pe, lambda: self.nc.tensor.transpose(out, in_, ident), r=r, w=w, signal=signal)

    def actf(self, out, in_, func, bias=None, scale=None, r=(), w=(), waw=True):
        kw = {}
        if bias is not None:
            kw["bias"] = bias
        if scale is not None:
            kw["scale"] = scale
        return self.op(self.act, lambda: self.nc.scalar.activation(out=out, in_=in_, func=func, **kw), r=r, w=w,
                       waw=waw)

    def tt(self, eng, out, in0, in1, op, r=(), w=()):
        return self.op(eng, lambda: eng.h.tensor_tensor(out=out, in0=in0, in1=in1, op=op), r=r, w=w)

    def ts(self, eng, out, in0, s1, s2, op0, op1=None, r=(), w=(), waw=True):
        if op1 is None:
            return self.op(eng, lambda: eng.h.tensor_scalar(out=out, in0=in0, scalar1=s1, scalar2=None, op0=op0),
                           r=r, w=w, waw=waw)
        return self.op(eng, lambda: eng.h.tensor_scalar(out=out, in0=in0, scalar1=s1, scalar2=s2, op0=op0, op1=op1),
                       r=r, w=w, waw=waw)

    def stt(self, out, in0, scalar, in1, op0, op1, r=(), w=()):
        return self.op(self.dve, lambda: self.nc.vector.scalar_tensor_tensor(
            out=out, in0=in0, scalar=scalar, in1=in1, op0=op0, op1=op1), r=r, w=w)

    def copy(self, eng, out, in_, r=(), w=()):
        if eng is self.act:
            return self.op(eng, lambda: self.nc.scalar.activation(out=out, in_=in_, func=AF.Copy), r=r, w=w)
        return self.op(eng, lambda: eng.h.tensor_copy(out=out, in_=in_), r=r, w=w)

    def memset(self, eng, ap, val, w=()):
        return self.op(eng, lambda: eng.h.memset(ap, val), w=w)


class Pool:
    def __init__(self, items):
        self.items = items
        self.i = 0

    def next(self):
        it = self.items[self.i % len(self.items)]
        self.i += 1
        return it


def build_program():
    nc = bass.Bass("TRN2", target_bir_lowering=False)
    dt = nc.dram_tensor

    def din(name, shape):
        return dt(name, list(shape), F32, kind="ExternalInput").ap()

    x_d = din("x", [NB, S, D])
    cT_d = din("cT", [P, 16])
    wada_d = din("wada", [48, P, 1024])
    badaT_d = din("badaT", [P, 48])
    w128_d = din("w128", [20, P, 1024])
    w256_d = din("w256", [8, P, 2048])
    wgate_d = din("wgate", [P, 128])
    wqk_d = din("wqk", [4, P, 512])
    wd1a_d = din("wd1a", [8, P, 1536])
    wd1b_d = din("wd1b", [8, P, 2048])
    wout_d = din("wout", [P, 8192])
    wfin_d = [din("wfin%d" % g, [P, 8 * 2 * n * P]) for g, n in enumerate(FG_SIZES)]
    wfdn_d = [din("wfdn%d" % g, [P, n * D]) for g, n in enumerate(FG_SIZES)]
    bfm_d = din("bfm", [P, 40])
    bgate_d = din("bgate", [P, 256])
    btok_d = din("btok", [P, 2560])
    convw_d = din("convw", [P, 32])
    convb_d = din("convb", [P, 8])
    mhg_d = din("mhg", [P, 8])
    lng_d = din("lng", [4, P, D])
    consts_d = din("consts", [P, 640])
    out_d = dt("out", [NB, S, D], F32, kind="ExternalOutput").ap()

    with ExitStack() as es:
        K = KB(nc, es)
        PE, ACT, DVE, POOL, SP = K.pe, K.act, K.dve, K.pool, K.sp

        def sb(name, shape, dtype=F32):
            return es.enter_context(nc.sbuf_tensor("sb_" + name, list(shape), dtype))

        ARENA_W = 152 * 256
        arena = sb("arena", [P, ARENA_W])

        def av(off_b, nbytes, dtype, pat=None, **kw):
            a = arena[:, off_b // 4:(off_b + nbytes) // 4]
            if dtype is BF16:
                a = a.bitcast(BF16)
            if pat is not None:
                a = a.rearrange(pat, **kw)
            return a

        KB_ = 1024
        hT = av(0, 32 * KB_, BF16, "p (c t) -> p c t", c=8)
        attT = av(32 * KB_, 16 * KB_, BF16, "p (c t) -> p c t", c=4)
        hmT = av(48 * KB_, 32 * KB_, BF16, "p (c t) -> p c t", c=8)
        TR = 80 * KB_
        yT = av(120 * KB_, 32 * KB_, BF16, "p (c t) -> p c t", c=8)
        accT = av(32 * KB_, 64 * KB_, F32, "p (c t) -> p c t", c=8)
        lnrep = av(112 * KB_, 8 * KB_, F32, "p (c t) -> p c t", c=2)
        actb = av(96 * KB_, 16 * KB_, BF16, "p (c t) -> p c t", c=4)
        o = TR
        QT = av(o, 4096, BF16); o += 4096
        KTh = [av(o, 4096, BF16), av(o + 4096, 4096, BF16)]; o += 8192
        VX = av(o, 4224, BF16, "p (j h e) -> p j h e", j=16, h=2); o += 4224
        PTs = [av(o + i * 1024, 1024, BF16) for i in range(4)]; o += 4096
        biasTab = av(o, 8192, F32, "p (h i j) -> p h i j", h=8, i=16); o += 8192
        bfv = av(o, 2048, F32); o += 2048
        attok = av(o, 1024, BF16, "p (i e) -> p i e", i=4); o += 1024
        o = TR
        uT = av(o, 8192, BF16, "p (c t) -> p c t", c=2); o += 8192
        qT = av(o, 4096, BF16); o += 4096
        kT = av(o, 4096, BF16); o += 4096
        khat = av(o, 4096, BF16, "p (j e) -> p j e", j=16); o += 4096
        vext = av(o, 8256, BF16, "p (j e) -> p j e", j=16); o += 8256
        Sb = av(o, 8256, BF16, "p (j e) -> p j e", j=16); o += 8256
        Sf = [av(o + i * 1032, 1028, F32) for i in range(2)]; o += 2064
        muT = [av(o + i * 4112, 4104, BF16) for i in range(2)]; o += 8224
        Dg = av(o, 2048, BF16, "p (c j e) -> p c j e", c=2, j=4); o += 2048
        bmvo = av(o, 4096, F32); o += 4096
        mhg = av(o, 32, F32); o += 32
        cw = [av(o + i * 1024, 1024, F32) for i in range(6)]; o += 6144
        cwb = [av(o + i * 512, 512, BF16) for i in range(2)]; o += 1024
        sdtb = [av(o + i * 256, 256, BF16) for i in range(2)]; o += 512
        assert o <= 152 * KB_, o
        d1t = [av(TR + i * 2048, 2048, F32) for i in range(8)]
        winb = [av(112 * KB_ + i * 16 * KB_, 16 * KB_, BF16, "p (k n) -> p k n", k=8) for i in range(2)]

        wpool_t = sb("wpool", [P, 4, 2048], BF16)
        wdnb = [av(144 * KB_, 8 * KB_, BF16, "p (f n) -> p f n", f=4),
                wpool_t[:, 0:2, :].rearrange("p a (b n) -> p (a b) n", b=2)]
        wout_v = av(96 * KB_, 16 * KB_, BF16)
        yog = wpool_t[:, :, :].rearrange("p a n -> p (a n)").bitcast(F32).rearrange("p (c t) -> p c t", c=8)

        consts = sb("consts", [P, 640])
        ident = consts[:, 0:128]
        triU = consts[:, 128:256]
        ones = consts[:, 256:384]
        mask01 = consts[:, 512:640]
        cbf = sb("cbf", [P, 640], BF16)
        identb = cbf[:, 0:128]
        mnegb = cbf[:, 128:256]
        zerob = cbf[:, 256:640]
        xts = [sb("xt%d" % i, [P, D]) for i in range(3)]
        xns = [sb("xn%d" % i, [P, D]) for i in range(2)]
        small = sb("small", [P, 64, 16])
        cTt = sb("cTt", [P, 16])
        siluT = sb("siluT", [P, 8, 2], BF16)
        badaT = sb("badaT", [P, 48])
        modT = sb("modT", [P, 48, 2])
        sc1p = sb("sc1p", [P, 8, 2])
        sc2p = sb("sc2p", [P, 8, 2])
        bfm = sb("bfm", [P, 40])
        bq8 = sb("bq8", [P, 4])
        bgate = sb("bgate", [P, 256])
        convw = sb("convw", [P, 32])
        convb = sb("convb", [P, 8])
        wg = sb("wg", [P, 128], BF16)
        gl = sb("gl", [P, 16, 16])
        nl = sb("nl", [P, 16, 16])
        nlf = sb("nlf", [P, 16, 8])
        nlm = sb("nlm", [P, 16, 4])
        igm = sb("igm", [P, 16, 4])
        Gt = sb("Gt", [P, 16, 8])
        Gincl = sb("Gincl", [P, 16, 8])
        Gf = sb("Gf", [P, 16, 8])
        colfac = sb("colfac", [P, 16, 4])
        ahat = sb("ahat", [P, 16, 4])
        enegb = sb("enegb", [P, 16, 4])
        EGt = sb("EGt", [P, 16, 4])
        enegb2 = sb("enegb2", [P, 16, 4])
        gtmp = sb("gtmp", [P, 16, 4])
        rec4 = sb("rec4", [P, 8])

        psum = [es.enter_context(nc.psum_tensor("ps%d" % i, [P, 512], F32)) for i in range(8)]
        _bank = [(psum[i], Reg("ps%d" % i)) for i in range(8)]
        PS = Pool(_bank[2:8])
        PSA = Pool(_bank[0:2])
        PSO = Pool(_bank[0:3])
        PSC = Pool(_bank[3:8])

        R = lambda n: Reg(n)
        r_consts, r_cbf = R("consts"), R("cbf")
        r_hT = [R("hT%d" % i) for i in range(NTC)]
        r_attT = [R("attT%d" % i) for i in range(NTC)]
        r_hmT = [R("hmT%d" % i) for i in range(NTC)]
        r_yT = [R("yT%d" % i) for i in range(NTC)]
        r_acc = [R("acc%d" % i) for i in range(NTC)]
        r_h2T = r_hT
        r_mod = R("mod")
        r_misc = R("misc")
        r_gates = R("gates")
        XT = Pool([(xts[i], R("xt%d" % i)) for i in range(3)])
        XN = Pool([(xns[i], R("xn%d" % i)) for i in range(2)])
        WP = Pool([(wpool_t[:, i, :], R("wp%d" % i)) for i in range(4)])
        PTP = Pool([(PTs[i], R("pt%d" % i)) for i in range(4)])
        small_i = [0]

        def sm():
            i = small_i[0] % 64
            small_i[0] += 1
            return small[:, i, :]
        r_small = [R("sm%d" % i) for i in range(64)]

        def smr():
            i = small_i[0] % 64
            small_i[0] += 1
            return small[:, i, :], r_small[i]

        class WPre:
            def __init__(self):
                self.plan, self.pos, self.issued = [], 0, []

            def set_plan(self, items):
                assert not self.issued and self.pos == len(self.plan)
                self.plan, self.pos = items, 0
                self.top_up()

            def top_up(self, n=3):
                while len(self.issued) < n and self.pos < len(self.plan):
                    cols, src = self.plan[self.pos]
                    wt, rw = WP.next()
                    wload(wt[:, 0:cols], src, [rw])
                    self.issued.append((wt, rw))
                    self.pos += 1

            def get(self):
                assert self.issued
                return self.issued.pop(0)

        wdma_q = [0]

        def wload(dst, src, w, **kw):
            K.dma(POOL, dst, src, w=w, max_dma_last_dim=4096, **kw)

        WPF = WPre()

        def dump(name, ap, shape, dtype, r):
            t = dt("dbg_" + name, list(shape), dtype, kind="ExternalOutput").ap()
            K.dma(SP, t, ap, r=r, is_out=True)
            _DUMPS.append("dbg_" + name)

        def ln_stats(src, r_src, width=D, with_act=True):
            s, rs = smr()
            nch = width // 512 if width >= 512 else 1
            cw_ = width // nch
            for c in range(nch):
                K.op(DVE, lambda c=c: nc.vector.bn_stats(out=s[:, c * 6:(c + 1) * 6], in_=src[:, c * cw_:(c + 1) * cw_]),
                     r=[r_src], w=[rs])
            K.op(DVE, lambda: nc.vector.bn_aggr(out=s[:, 12:14], in_=s[:, 0:6 * nch]), r=[rs], w=[rs])
            if not with_act:
                return s, rs
            rstd_act(s, rs, EPS)
            return s[:, 15:16], s[:, 14:15], rs

        def rstd_act(s, rs, bias, in_ap=None, scale=1.0):
            K.actf(s[:, 0:1], s[:, 13:14] if in_ap is None else in_ap, AF.Ln, bias=bias, scale=scale, r=[rs], w=[rs])
            K.actf(s[:, 15:16], s[:, 0:1], AF.Exp, scale=-0.5, r=[rs], w=[rs])
            K.op(ACT, lambda: nc.scalar.mul(out=s[:, 1:2], in_=s[:, 15:16], mul=-1.0), r=[rs], w=[rs])
            K.actf(s[:, 14:15], s[:, 12:13], AF.Identity, scale=s[:, 1:2], r=[rs], w=[rs])

        K.dma(SP, consts[:], consts_d[:, :], w=[r_consts])
        K.dma(SP, cTt[:], cT_d[:, :], w=[r_misc])
        K.dma(SP, badaT[:], badaT_d[:, :], w=[r_misc])
        K.dma(SP, bfm[:], bfm_d[:, :], w=[r_misc])
        K.dma(SP, bgate[:], bgate_d[:, :], w=[r_misc])
        K.dma(SP, convw[:], convw_d[:, :], w=[r_misc])
        K.dma(SP, convb[:], convb_d[:, :], w=[r_misc])
        r_wg = R("wg")
        wload(wg[:], wgate_d[:, :], [r_wg])
        K.copy(DVE, identb, ident, r=[r_consts], w=[r_cbf])
        K.copy(DVE, mnegb, consts[:, 384:512], r=[r_consts], w=[r_cbf])
        K.memset(DVE, zerob, 0.0, w=[r_cbf])
        K.ts(DVE, bq8[:], bfm[:, 0:4], 0.125, None, ALU.mult, r=[r_misc], w=[r_misc])
        K.actf(siluT[:].rearrange("p a b -> p (a b)"), cTt[:], AF.Silu, r=[r_misc], w=[r_misc])

        psm, r_psm = PS.next()
        WPF.set_plan([(1024, wada_d[g]) for g in range(48)])
        for g in range(48):
            wt, rw = WPF.get()
            for kc in range(8):
                K.mm(psm[:, g * 2:g * 2 + 2], wt[:, kc * 128:(kc + 1) * 128], siluT[:, kc, :],
                     start=(kc == 0), stop=(kc == 7), r=[rw, r_misc], w=[r_psm])
            WPF.top_up()
        psm3 = psm[:, 0:96].rearrange("p (g b) -> p g b", b=2)
        for b in range(2):
            K.tt(DVE, modT[:, :, b], psm3[:, :, b], badaT[:], ALU.add, r=[r_psm, r_misc], w=[r_mod])
        K.ts(DVE, sc1p[:], modT[:, 8:16, :], 1.0, None, ALU.add, r=[r_mod], w=[r_mod])
        K.ts(DVE, sc2p[:], modT[:, 32:40, :], 1.0, None, ALU.add, r=[r_mod], w=[r_mod])
        sh1 = modT[:, 0:8, :]
        g1 = modT[:, 16:24, :]
        sh2 = modT[:, 24:32, :]
        g2 = modT[:, 40:48, :]
        if DEBUG_STAGE == "mod":
            dump("mod", modT[:].rearrange("p a b -> p (a b)"), [P, 96], F32, [r_mod])

        evac_i = [0]

        def evac_eng():
            evac_i[0] += 1
            return ACT if evac_i[0] % 2 else DVE

        def affine_evac(out, in_, scale_ap, bias_ap, r, w, eng=None):
            e = eng if eng is not None else evac_eng()
            if e is ACT:
                K.actf(out, in_, AF.Identity, bias=bias_ap, scale=scale_ap, r=r, w=w, waw=True)
            else:
                K.ts(DVE, out, in_, scale_ap, bias_ap, ALU.mult, ALU.add, r=r, w=w, waw=True)

        def ln_norm(src, r_src, rstd, nmr, rs):
            xn, r_xn = XN.next()
            K.actf(xn[:], src[:], AF.Identity, bias=nmr, scale=rstd, r=[r_src, rs], w=[r_xn])
            return xn, r_xn

        def ln_T(b, blk, xn, r_xn, dstT, r_dst, scp, shf):
            for half in range(2):
                ps, rp = PS.next()
                for f in range(4):
                    fc = half * 4 + f
                    K.tr(ps[:, f * 128:(f + 1) * 128], xn[:, fc * 128:(fc + 1) * 128], ident,
                         r=[r_xn, r_consts], w=[rp], signal=(f == 3))
                for f in range(4):
                    fc = half * 4 + f
                    affine_evac(dstT[:, fc, blk * 128:(blk + 1) * 128], ps[:, f * 128:(f + 1) * 128],
                                scp[:, fc, b:b + 1], shf[:, fc, b:b + 1], r=[rp, r_mod], w=[r_dst],
                                eng=(ACT if half == 0 else DVE))

        for b in range(NB):
            plan = []
            for hp_ in range(4):
                plan += [(1024, w128_d[hp_]), (1024, w128_d[4 + hp_]), (1024, w128_d[8 + hp_])]
            for h_ in range(4):
                plan += [(1024, w128_d[12 + 2 * h_]), (1024, w128_d[13 + 2 * h_]), (512, wqk_d[h_]),
                         (2048, w256_d[h_]), (2048, w256_d[4 + h_])]
            for dm_ in range(8):
                plan += [(1536, wd1a_d[dm_]), (2048, wd1b_d[dm_])]
            WPF.set_plan(plan)
            K.mark("A%d" % b)
            st_a = {}

            xld = {}

            def a_0(blk):
                xt, r_xt = XT.next()
                K.dma(SP, xt[:], x_d[b, blk * 128:(blk + 1) * 128, :], w=[r_xt])
                xld[blk] = (xt, r_xt)

            def a_1(blk):
                xt, r_xt = xld.pop(blk)
                sA, rsA = ln_stats(xt, r_xt, with_act=False)
                st_a[blk] = (xt, r_xt, sA, rsA)

            def a_2(blk):
                xt, r_xt, sA, rsA = st_a[blk]
                rstd_act(sA, rsA, EPS)
                st_a[blk] = ln_norm(xt, r_xt, sA[:, 15:16], sA[:, 14:15], rsA)

            def a_3(blk):
                xn, r_xn = st_a.pop(blk)
                ln_T(b, blk, xn, r_xn, hT, r_hT[blk // 4], sc1p, sh1)

            a_0(0)
            for t in range(NBLK + 2):
                if 0 <= t - 1 < NBLK:
                    a_2(t - 1)
                if t + 1 < NBLK:
                    a_0(t + 1)
                if t < NBLK:
                    a_1(t)
                if 0 <= t - 2 < NBLK:
                    a_3(t - 2)
            if DEBUG_STAGE == "A" and b == 0:
                dump("hT", hT.rearrange("p c t -> p (c t)"), [P, 8 * S], BF16, r_hT)
                break

            K.mark("B0%d" % b)
            psg, r_psg = PS.next()
            for j in range(NBLK):
                for kc in range(8):
                    K.mm(psg[:, j * 16:(j + 1) * 16], hT[:, kc, j * 128:(j + 1) * 128], wg[:, kc * 16:(kc + 1) * 16],
                         start=(kc == 0), stop=(kc == 7), r=[r_hT[j // 4], r_wg], w=[r_psg])
            glf = gl[:].rearrange("p a b -> p (a b)")
            nlf_ = nl[:].rearrange("p a b -> p (a b)")
            K.tt(DVE, glf, psg[:, 0:256], bgate[:], ALU.add, r=[r_psg, r_misc], w=[r_gates])
            K.actf(nlf_, glf, AF.Exp, scale=-1.0, r=[r_gates], w=[r_gates])
            K.actf(nlf_, nlf_, AF.Ln, bias=1.0, scale=1.0, r=[r_gates], w=[r_gates])
            K.copy(DVE, nlf[:], nl[:, :, 0:8], r=[r_gates], w=[r_gates])
            K.copy(DVE, nlm[:], nl[:, :, 12:16], r=[r_gates], w=[r_gates])
            K.copy(DVE, igm[:], gl[:, :, 8:12], r=[r_gates], w=[r_gates])
            psw, r_psw = PS.next()
            K.mm(psw[:, 0:128], triU, nlf[:].rearrange("p a b -> p (a b)"), True, True, r=[r_gates, r_consts], w=[r_psw])
            K.mm(psw[:, 128:256], ones, nlf[:].rearrange("p a b -> p (a b)"), True, True, r=[r_gates, r_consts], w=[r_psw])
            K.mm(psw[:, 256:320], triU, nlm[:].rearrange("p a b -> p (a b)"), True, True, r=[r_gates, r_consts], w=[r_psw])
            K.mm(psw[:, 320:384], ones, nlm[:].rearrange("p a b -> p (a b)"), True, True, r=[r_gates, r_consts], w=[r_psw])
            Gtf = Gt[:].rearrange("p a b -> p (a b)")
            K.copy(DVE, Gtf, psw[:, 128:256], r=[r_psw], w=[r_gates])
            K.copy(DVE, Gincl[:, 0, :], Gt[:, 0, :], r=[r_gates], w=[r_gates])
            for j in range(1, NBLK):
                K.tt(DVE, Gincl[:, j, :], Gincl[:, j - 1, :], Gt[:, j, :], ALU.add, r=[r_gates], w=[r_gates])
            Gff = Gf[:].rearrange("p a b -> p (a b)")
            K.tt(DVE, Gff, psw[:, 0:128], Gincl[:].rearrange("p a b -> p (a b)"), ALU.add, r=[r_psw, r_gates], w=[r_gates])
            K.tt(DVE, Gff, Gff, Gtf, ALU.subtract, r=[r_gates], w=[r_gates])
            r_bt = R("biasTab")
            for h in range(8):
                for qc_ in range(NTC):
                    i = 4 * qc_ + 3
                    K.ts(DVE, biasTab[:, h, qc_, 0:i + 1], Gf[:, 0:i + 1, h], Gincl[:, i, h:h + 1], None, ALU.subtract,
                         r=[r_gates], w=[r_bt])
            gt_ = gtmp[:].rearrange("p a b -> p (a b)")
            K.tt(DVE, gt_, psw[:, 256:320], igm[:].rearrange("p a b -> p (a b)"), ALU.add, r=[r_psw, r_gates], w=[r_gates])
            K.actf(colfac[:].rearrange("p a b -> p (a b)"), gt_, AF.Exp, r=[r_gates], w=[r_gates])
            K.actf(enegb[:].rearrange("p a b -> p (a b)"), psw[:, 256:320], AF.Exp, r=[r_psw], w=[r_gates])
            K.actf(EGt[:].rearrange("p a b -> p (a b)"), psw[:, 320:384], AF.Exp, scale=-1.0, r=[r_psw], w=[r_gates])
            K.tt(DVE, enegb2[:].rearrange("p a b -> p (a b)"), enegb[:].rearrange("p a b -> p (a b)"),
                 enegb[:].rearrange("p a b -> p (a b)"), ALU.mult, r=[r_gates], w=[r_gates])
            K.tt(DVE, gt_, gt_, psw[:, 320:384], ALU.subtract, r=[r_psw, r_gates], w=[r_gates])
            K.actf(ahat[:].rearrange("p a b -> p (a b)"), gt_, AF.Exp, r=[r_gates], w=[r_gates])
            K.dma(SP, bfv, btok_d[:, 0:512], w=[r_bt])
            if DEBUG_STAGE == "B0" and b == 0:
                dump("G", Gff, [P, 128], F32, [r_gates])
                dump("colfac", colfac[:].rearrange("p a b -> p (a b)"), [P, 64], F32, [r_gates])
                dump("ahat", ahat[:].rearrange("p a b -> p (a b)"), [P, 64], F32, [r_gates])
                dump("EG", EGt[:].rearrange("p a b -> p (a b)"), [P, 64], F32, [r_gates])
                break

            K.mark("B%d" % b)
            r_QT, r_KT, r_VX, r_attok = R("QT"), R("KT"), R("VX"), R("attok")
            K.memset(DVE, VX[:, :, :, 64:65], 1.0, w=[r_VX])
            K.memset(DVE, KTh[0], 0.0, w=[r_KT])
            K.memset(POOL, KTh[1], 0.0, w=[r_KT])
            for hp in range(4):
                wq, rwq = WPF.get()
                wk, rwk = WPF.get()
                wv, rwv = WPF.get()
                for tc in range(NTC):
                    ps, rp = PS.next()
                    for kc in range(8):
                        K.mm(ps[:, :], wq[:, kc * 128:(kc + 1) * 128], hT[:, kc, tc * 512:(tc + 1) * 512],
                             start=(kc == 0), stop=(kc == 7), r=[rwq, r_hT[tc]], w=[rp])
                    K.actf(QT[:, tc * 512:(tc + 1) * 512], ps[:, :], AF.Identity, bias=bq8[:, hp:hp + 1], scale=0.125,
                           r=[rp, r_misc], w=[r_QT])
                    ps, rp = PS.next()
                    for kc in range(8):
                        K.mm(ps[:, :], wk[:, kc * 128:(kc + 1) * 128], hT[:, kc, tc * 512:(tc + 1) * 512],
                             start=(kc == 0), stop=(kc == 7), r=[rwk, r_hT[tc]], w=[rp])
                    for hl_ in range(2):
                        pr = slice(hl_ * 64, (hl_ + 1) * 64)
                        K.ts(DVE, KTh[hl_][pr, tc * 512:(tc + 1) * 512], ps[pr, :], bfm[pr, 4 + hp:5 + hp], None, ALU.add,
                             r=[rp, r_misc], w=[r_KT])
                for j in range(NBLK):
                    ps, rp = PS.next()
                    jl = 0
                    for kc in range(8):
                        K.mm(ps[:, jl * 128:(jl + 1) * 128], hT[:, kc, j * 128:(j + 1) * 128], wv[:, kc * 128:(kc + 1) * 128],
                             start=(kc == 0), stop=(kc == 7), r=[rwv, r_hT[j // 4]], w=[rp])
                    K.tt(DVE, VX[:, j, :, 0:64], ps[:, jl * 128:(jl + 1) * 128].rearrange("p (h e) -> p h e", h=2),
                         bfv[:, hp * 128:(hp + 1) * 128].rearrange("p (h e) -> p h e", h=2), ALU.add,
                         r=[rp, r_bt], w=[r_VX])
                WPF.top_up()
                for qc in range(NTC):
                    for hl in range(2):
                        h = hp * 2 + hl
                        pv, r_pv = PSA.next()
                        K.mm(pv[:, 0:264], zerob[0:1, 0:128], zerob[0:1, 0:264], True, False, r=[r_cbf], w=[r_pv],
                             signal=True, sgc=True)
                        nst = 4 * qc + 4
                        sts = {}

                        def emitS(j):
                            a = max(0, j - 4 * qc)
                            st, r_st = PS.next()
                            diag = j >= 4 * qc
                            K.mm(st[:, a * 128:512], KTh[hl][:, j * 128:(j + 1) * 128],
                                 QT[:, qc * 512 + a * 128:(qc + 1) * 512],
                                 start=True, stop=(not diag), r=[r_KT, r_QT], w=[r_st])
                            if diag:
                                K.mm(st[:, a * 128:(a + 1) * 128], identb, mnegb, start=False, stop=True,
                                     r=[r_cbf], w=[r_st])
                            sts[j] = (st, r_st, a)

                        def emitEPV(j):
                            st, r_st, a = sts.pop(j)
                            pt, r_pt = PTP.next()
                            K.actf(pt[:, a * 128:512], st[:, a * 128:512], AF.Exp,
                                   bias=biasTab[:, h, qc, j:j + 1], scale=1.0, r=[r_st, r_bt], w=[r_pt])
                            for il in range(a, 4):
                                i = 4 * qc + il
                                K.mm(pv[:, il * 66:il * 66 + 65], pt[:, il * 128:(il + 1) * 128], VX[:, j, hl, 0:65],
                                     start=False, stop=(j == nst - 1 and il == 3), r=[r_pt, r_VX], w=[r_pv],
                                     signal=(il == 3), sgc=True)

                        emitS(0)
                        if nst > 1:
                            emitS(1)
                        for j in range(nst):
                            if j + 2 < nst:
                                emitS(j + 2)
                            emitEPV(j)
                        pv3 = pv[:, 0:264].rearrange("p (i e) -> p i e", e=66)
                        K.op(DVE, lambda: nc.vector.reciprocal(out=rec4[:, hl * 4:hl * 4 + 4], in_=pv3[:, :, 64]),
                             r=[r_pv], w=[r_misc])
                        for il in range(4):
                            K.ts(DVE, attok[:, il, hl * 64:(hl + 1) * 64], pv3[:, il, 0:64],
                                 rec4[:, hl * 4 + il:hl * 4 + il + 1], None, ALU.mult, r=[r_pv, r_misc], w=[r_attok])
                    pst, r_pst = PS.next()
                    pstb = pst[:, 0:256].bitcast(BF16)
                    for il in range(4):
                        K.tr(pstb[:, il * 128:(il + 1) * 128], attok[:, il, :], identb, r=[r_attok, r_cbf], w=[r_pst],
                             signal=(il == 3))
                    K.copy(ACT, attT[:, hp, qc * 512:(qc + 1) * 512], pstb, r=[r_pst], w=[r_attT[qc]])
            if DEBUG_STAGE == "B" and b == 0:
                dump("attT", attT.rearrange("p c t -> p (c t)"), [P, 4 * S], BF16, r_attT)
                break

            K.mark("C%d" % b)
            K.barrier()
            r_uT, r_qT, r_kT, r_khat, r_vext, r_Sb, r_Dg = R("uT"), R("qT"), R("kT"), R("khat"), R("vext"), R("Sb"), R("Dg")
            r_Sf = [R("Sf0"), R("Sf1")]
            r_mu = [R("mu0"), R("mu1")]
            r_cc = R("cconst")
            r_cw = [R("cw%d" % i) for i in range(6)]
            r_cwb = [R("cwb%d" % i) for i in range(2)]
            r_sdtb = [R("sdtb%d" % i) for i in range(2)]
            K.dma(SP, bmvo, btok_d[:, 512:1536], w=[r_cc])
            K.dma(SP, mhg, mhg_d[:, :], w=[r_cc])
            for i in range(2):
                K.memset(DVE, muT[i][:, 0:4], 0.0, w=[r_mu[i]])
            K.memset(DVE, vext[:, :, 256:257], 1.0, w=[r_vext])
            K.memset(DVE, Sb[:, 0, :], 0.0, w=[r_Sb])
            cwi = [0]
            for h in range(4):
                K.mark('c_mu%d_%d' % (b, h))
                for fcl in range(2):
                    fc = 2 * h + fcl
                    for jj in range(4):
                        K.ts(DVE, Dg[:, fcl, jj, :], ident, convw[:, jj * 8 + fc:jj * 8 + fc + 1], None, ALU.mult,
                             r=[r_consts, r_misc], w=[r_Dg])
                for fcl in range(2):
                    fc = 2 * h + fcl
                    wm, rwm = WPF.get()
                    mu, rmu = muT[fcl], r_mu[fcl]
                    for tc in range(NTC):
                        ps, rp = PS.next()
                        for kc in range(8):
                            K.mm(ps[:, :], wm[:, kc * 128:(kc + 1) * 128], hT[:, kc, tc * 512:(tc + 1) * 512],
                                 start=(kc == 0), stop=(kc == 7), r=[rwm, r_hT[tc]], w=[rp])
                        K.actf(mu[:, 3 + tc * 512:3 + (tc + 1) * 512], ps[:, :], AF.Identity, bias=bfm[:, 8 + fc:9 + fc],
                               scale=1.0, r=[rp, r_misc], w=[rmu])
                    for tc in range(NTC):
                        ps, rp = PS.next()
                        for jj in range(4):
                            K.mm(ps[:, :], Dg[:, fcl, jj, :], mu[:, tc * 512 + jj:tc * 512 + jj + 512],
                                 start=(jj == 0), stop=(jj == 3), r=[r_Dg, rmu], w=[rp])
                        K.actf(uT[:, fcl, tc * 512:(tc + 1) * 512], ps[:, :], AF.Silu, bias=convb[:, fc:fc + 1], scale=1.0,
                               r=[rp, r_misc], w=[r_uT])
                K.mark('c_qk%d_%d' % (b, h))
                WPF.top_up()
                wqk, rwqk = WPF.get()
                for tc in range(NTC):
                    ps, rp = PS.next()
                    for ic in range(2):
                        K.mm(ps[:, :], wqk[:, ic * 128:(ic + 1) * 128], uT[:, ic, tc * 512:(tc + 1) * 512],
                             start=(ic == 0), stop=(ic == 1), r=[rwqk, r_uT], w=[rp])
                    K.op(ACT, lambda ps=ps, tc=tc: nc.scalar.mul(out=qT[:, tc * 512:(tc + 1) * 512], in_=ps[:, :], mul=128.0 ** -0.5),
                         r=[rp], w=[r_qT])
                    ps, rp = PS.next()
                    for ic in range(2):
                        K.mm(ps[:, :], wqk[:, 256 + ic * 128:256 + (ic + 1) * 128], uT[:, ic, tc * 512:(tc + 1) * 512],
                             start=(ic == 0), stop=(ic == 1), r=[rwqk, r_uT], w=[rp])
                    K.copy(DVE, kT[:, tc * 512:(tc + 1) * 512], ps[:, :], r=[rp], w=[r_kT])
                for j in range(NBLK):
                    ps, rp = PS.next()
                    jl = 0
                    for ic in range(2):
                        K.mm(ps[:, jl * 128:(jl + 1) * 128], uT[:, ic, j * 128:(j + 1) * 128],
                             wqk[:, 256 + ic * 128:256 + (ic + 1) * 128],
                             start=(ic == 0), stop=(ic == 1), r=[rwqk, r_uT], w=[rp])
                    K.ts(DVE, khat[:, j, :], ps[:, jl * 128:(jl + 1) * 128], ahat[:, j, h:h + 1], None, ALU.mult,
                         r=[rp, r_gates], w=[r_khat])
                K.mark('c_vext%d_%d' % (b, h))
                WPF.top_up()
                wmv, rwmv = WPF.get()
                for j in range(NBLK):
                    ps, rp = PS.next()
                    jl = 0
                    for kc in range(8):
                        K.mm(ps[:, jl * 256:(jl + 1) * 256], hT[:, kc, j * 128:(j + 1) * 128], wmv[:, kc * 256:(kc + 1) * 256],
                             start=(kc == 0), stop=(kc == 7), r=[rwmv, r_hT[j // 4]], w=[rp])
                    K.tt(DVE, vext[:, j, 0:256], ps[:, jl * 256:(jl + 1) * 256], bmvo[:, h * 256:(h + 1) * 256], ALU.add,
                         r=[rp, r_cc], w=[r_vext])
                K.mark('c_states%d_%d' % (b, h))
                WPF.top_up()
                K.memset(DVE, Sf[0], 0.0, w=[r_Sf[0]])
                for j in range(NBLK - 1):
                    ps, rp = PS.next()
                    K.mm(ps[:, 0:257], khat[:, j, :], vext[:, j, 0:257], True, True, r=[r_khat, r_vext], w=[rp])
                    cur, nxt = j % 2, (j + 1) % 2
                    K.stt(Sf[nxt], Sf[cur], EGt[:, j, h:h + 1], ps[:, 0:257], ALU.mult, ALU.add,
                          r=[r_Sf[cur], rp, r_gates], w=[r_Sf[nxt]])
                    K.copy(ACT, Sb[:, j + 1, 0:257], Sf[nxt], r=[r_Sf[nxt]], w=[r_Sb])
                K.mark('c_sig%d_%d' % (b, h))
                wmo, rwmo = WPF.get()
                sgT = uT
                for fcl in range(2):
                    fc = 2 * h + fcl
                    for tc in range(NTC):
                        ps, rp = PS.next()
                        for kc in range(8):
                            K.mm(ps[:, :], wmo[:, kc * 256 + fcl * 128:kc * 256 + (fcl + 1) * 128],
                                 hT[:, kc, tc * 512:(tc + 1) * 512],
                                 start=(kc == 0), stop=(kc == 7), r=[rwmo, r_hT[tc]], w=[rp])
                        K.actf(sgT[:, fcl, tc * 512:(tc + 1) * 512], ps[:, :], AF.Sigmoid, bias=bfm[:, 32 + fc:33 + fc], scale=1.0,
                               r=[rp, r_misc], w=[r_uT])
                        K.ts(DVE, sgT[:, fcl, tc * 512:(tc + 1) * 512], sgT[:, fcl, tc * 512:(tc + 1) * 512],
                             mhg[:, fc:fc + 1], None, ALU.mult, r=[r_uT, r_cc], w=[r_uT])
                WPF.top_up()
                K.mark('c_loop%d_%d' % (b, h))
                st_c = {}

                def c_s1(j):
                    pss, rps = PSC.next()
                    K.mm(pss[:, 0:128], kT[:, j * 128:(j + 1) * 128], qT[:, j * 128:(j + 1) * 128], True, True,
                         r=[r_kT, r_qT], w=[rps])
                    sdt, r_sdt = sdtb[j % 2], r_sdtb[j % 2]
                    K.stt(sdt, pss[:, 0:128], colfac[:, j, h:h + 1], mask01, ALU.mult, ALU.mult,
                          r=[rps, r_gates, r_consts], w=[r_sdt])
                    po, rpo = PSO.next()
                    K.mm(po[:, 0:257], qT[:, j * 128:(j + 1) * 128], Sb[:, j, 0:257], True, False, r=[r_qT, r_Sb], w=[rpo],
                         signal=False)
                    K.mm(po[:, 0:257], sdt, vext[:, j, 0:257], False, True, r=[r_sdt, r_vext], w=[rpo])
                    st_c[j] = (po, rpo)

                def c_s2(j):
                    po, rpo = st_c[j]
                    s, rs = smr()
                    K.ts(DVE, s[:, 4:5], po[:, 256:257], po[:, 256:257], enegb2[:, j, h:h + 1], ALU.mult, ALU.max,
                         r=[rpo, r_gates], w=[rs])
                    K.op(DVE, lambda: nc.vector.bn_stats(out=s[:, 6:12], in_=po[:, 0:256]), r=[rpo], w=[rs])
                    K.op(DVE, lambda: nc.vector.bn_aggr(out=s[:, 12:14], in_=s[:, 6:12]), r=[rs], w=[rs])
                    st_c[j] = (po, rpo, s, rs)

                def c_s2a(j):
                    po, rpo, s, rs = st_c[j]
                    rstd_act(s, rs, s[:, 13:14], in_ap=s[:, 4:5], scale=EPS)

                def c_s3(j):
                    po, rpo, s, rs = st_c[j]
                    hg, r_hg = cwb[j % 2], r_cwb[j % 2]
                    K.actf(hg, po[:, 0:256], AF.Identity, bias=s[:, 14:15], scale=s[:, 15:16], r=[rpo, rs], w=[r_hg])
                    st_c[j] = (hg, r_hg)

                def c_s3b(j):
                    hg, r_hg = st_c[j]
                    pst, r_pst = PSC.next()
                    pstb = pst[:, 0:128].bitcast(BF16)
                    for fcl in range(2):
                        K.tr(pstb[:, fcl * 128:(fcl + 1) * 128], hg[:, fcl * 128:(fcl + 1) * 128], identb,
                             r=[r_hg, r_cbf], w=[r_pst], signal=(fcl == 1))
                    st_c[j] = (pstb, r_pst)

                def c_s4(j):
                    pstb, r_pst = st_c.pop(j)
                    K.tt(DVE, hmT[:, 2 * h:2 * h + 2, j * 128:(j + 1) * 128], pstb.rearrange("p (c t) -> p c t", c=2),
                         sgT[:, :, j * 128:(j + 1) * 128], ALU.mult, r=[r_pst, r_uT], w=[r_hmT[j // 4]])

                okc = lambda x: 0 <= x < NBLK
                for t in range(NBLK + 3):
                    if okc(t - 1):
                        c_s2(t - 1)
                    if okc(t - 3):
                        c_s4(t - 3)
                    if okc(t - 2):
                        c_s3(t - 2)
                    if okc(t - 1):
                        c_s2a(t - 1)
                    if okc(t):
                        c_s1(t)
                    if okc(t - 2):
                        c_s3b(t - 2)
            if DEBUG_STAGE == "C" and b == 0:
                dump("hmT", hmT.rearrange("p c t -> p (c t)"), [P, 8 * S], BF16, r_hmT)
                break

            K.mark("D1%d" % b)
            K.barrier()
            r_d1 = [R("d1t%d" % i) for i in range(8)]
            r_wo = R("wo")
            for i in range(4):
                wload(wout_v[:, i * 2048:(i + 1) * 2048], wout_d[:, i * 2048:(i + 1) * 2048], [r_wo])
            d1i = [0]
            for dm in range(8):
                wa, rwa = WPF.get()
                wb_, rwb = WPF.get()
                WPF.top_up(2)
                for tc in range(NTC):
                    sl = slice(tc * 512, (tc + 1) * 512)
                    psga, rpga = PS.next()
                    for kc in range(8):
                        K.mm(psga[:, :], wb_[:, kc * 128:(kc + 1) * 128], hT[:, kc, sl], start=(kc == 0), stop=(kc == 7),
                             r=[rwb, r_hT[tc]], w=[rpga])
                    psgb, rpgb = PS.next()
                    for kc in range(8):
                        K.mm(psgb[:, :], wb_[:, 1024 + kc * 128:1024 + (kc + 1) * 128], hT[:, kc, sl], start=(kc == 0),
                             stop=(kc == 7), r=[rwb, r_hT[tc]], w=[rpgb])
                    psa, rpa = PS.next()
                    for ic in range(4):
                        K.mm(psa[:, :], wa[:, ic * 128:(ic + 1) * 128], attT[:, ic, sl], start=(ic == 0), stop=(ic == 3),
                             r=[rwa, r_attT[tc]], w=[rpa])
                    psb, rpb = PS.next()
                    for ic in range(8):
                        K.mm(psb[:, :], wa[:, 512 + ic * 128:512 + (ic + 1) * 128], hmT[:, ic, sl], start=(ic == 0),
                             stop=(ic == 7), r=[rwa, r_hmT[tc]], w=[rpb])
                    i0 = (d1i[0] % 2) * 4
                    d1i[0] += 1
                    sga, sgb, t1, t2 = d1t[i0], d1t[i0 + 1], d1t[i0 + 2], d1t[i0 + 3]
                    K.actf(sga, psga[:, :], AF.Sigmoid, bias=bfm[:, 16 + dm:17 + dm], scale=1.0, r=[rpga, r_misc], w=[r_d1[i0]])
                    K.actf(sgb, psgb[:, :], AF.Sigmoid, bias=bfm[:, 24 + dm:25 + dm], scale=1.0, r=[rpgb, r_misc], w=[r_d1[i0 + 1]])
                    K.tt(DVE, t1, psa[:, :], sga, ALU.mult, r=[rpa, r_d1[i0]], w=[r_d1[i0 + 2]])
                    K.tt(DVE, t2, psb[:, :], sgb, ALU.mult, r=[rpb, r_d1[i0 + 1]], w=[r_d1[i0 + 3]])
                    K.tt(POOL, yT[:, dm, sl], t1, t2, ALU.add, r=[r_d1[i0 + 2], r_d1[i0 + 3]], w=[r_yT[tc]])
                WPF.top_up()
            if DEBUG_STAGE == "D1" and b == 0:
                dump("yT", yT.rearrange("p c t -> p (c t)"), [P, 8 * S], BF16, r_yT)
                break

            K.mark("D2%d" % b)
            K.barrier()
            r_yog, r_ln = R("yog"), R("lnrep")
            h2T = hT
            K.dma(SP, lnrep[:, 0, :], lng_d[0], w=[r_ln])
            K.dma(SP, lnrep[:, 1, :], lng_d[1], w=[r_ln])
            r_win = [R("win0"), R("win1")]
            r_wdn = [R("wdn0"), R("wdn1")]
            r_actb = R("actb")
            r_sg = [R("sg0"), R("sg1")]
            sgt = [xns[0][:, 0:512], xns[1][:, 0:512]]
            sgi = [0]

            def ffn_load(g, extra=()):
                n = FG_SIZES[g]
                wi = winb[(g + 1) % 2]
                for kc in range(8):
                    wload(wi[:, kc, 0:2 * n * 128], wfin_d[g][:, kc * 2 * n * 128:(kc + 1) * 2 * n * 128],
                          [r_win[(g + 1) % 2]] + list(extra))
                wd = wdnb[g % 2]
                for f in range(n):
                    wload(wd[:, f, :], wfdn_d[g][:, f * D:(f + 1) * D], [r_wdn[g % 2]] + list(extra))

            st_d = {}

            def d_wout(tc):
                sl = slice(tc * 512, (tc + 1) * 512)
                for dm in range(8):
                    ps, rp = PS.next()
                    for ic in range(8):
                        K.mm(ps[:, :], wout_v[:, ic * 1024 + dm * 128:ic * 1024 + (dm + 1) * 128], yT[:, ic, sl],
                             start=(ic == 0), stop=(ic == 7), r=[r_wo, r_yT[tc]], w=[rp])
                    K.actf(yog[:, dm, :], ps[:, :], AF.Identity, scale=g1[:, dm, b:b + 1], r=[rp, r_mod], w=[r_yog])

            def d_a(blk):
                tc, tl = blk // 4, blk % 4
                if tl == 0:
                    d_wout(tc)
                xt, r_xt = xld2.pop(blk)
                for half in range(2):
                    ps, rp = PS.next()
                    for f in range(4):
                        dm = half * 4 + f
                        K.tr(ps[:, f * 128:(f + 1) * 128], yog[:, dm, tl * 128:(tl + 1) * 128], ident,
                             r=[r_yog, r_consts], w=[rp], signal=(f == 3))
                    K.stt(xt[:, half * 512:(half + 1) * 512], xt[:, half * 512:(half + 1) * 512], ALPHA, ps[:, :],
                          ALU.mult, ALU.add, r=[r_xt, rp], w=[r_xt])
                sA, rsA = ln_stats(xt, r_xt, with_act=False)
                st_d[blk] = (xt, r_xt, sA, rsA)

            xld2 = {}

            def d_0(blk):
                xt, r_xt = XT.next()
                K.dma(SP, xt[:], x_d[b, blk * 128:(blk + 1) * 128, :], w=[r_xt])
                xld2[blk] = (xt, r_xt)

            def d_b(blk):
                xt, r_xt, sA, rsA = st_d[blk]
                rstd_act(sA, rsA, EPS)
                K.actf(xt[:], xt[:], AF.Identity, bias=sA[:, 14:15], scale=sA[:, 15:16], r=[r_xt, rsA], w=[r_xt])

            def d_c(blk):
                xt, r_xt, sA, rsA = st_d[blk]
                K.tt(DVE, xt[:], xt[:], lnrep[:, 0, :], ALU.mult, r=[r_xt, r_ln], w=[r_xt])
                K.tt(DVE, xt[:], xt[:], lnrep[:, 1, :], ALU.add, r=[r_xt, r_ln], w=[r_xt])

            def d_d(blk):
                tc = blk // 4
                xt, r_xt, sA, rsA = st_d[blk]
                for half in range(2):
                    ps, rp = PS.next()
                    for f in range(4):
                        dm = half * 4 + f
                        K.tr(ps[:, f * 128:(f + 1) * 128], xt[:, dm * 128:(dm + 1) * 128], ident,
                             r=[r_xt, r_consts], w=[rp], signal=(f == 3))
                    K.op(ACT, lambda ps=ps, half=half, blk=blk: nc.scalar.mul(
                        out=accT[:, half * 4:half * 4 + 4, blk * 128:(blk + 1) * 128],
                        in_=ps[:, :].rearrange("p (c t) -> p c t", c=4), mul=ALPHA), r=[rp], w=[r_acc[tc]])
                sB, rsB = ln_stats(xt, r_xt, with_act=False)
                st_d[blk] = (xt, r_xt, sB, rsB)

            def d_e(blk):
                xt, r_xt, sB, rsB = st_d[blk]
                rstd_act(sB, rsB, EPS)
                st_d[blk] = ln_norm(xt, r_xt, sB[:, 15:16], sB[:, 14:15], rsB)

            def d_f(blk):
                xn, r_xn = st_d.pop(blk)
                ln_T(b, blk, xn, r_xn, h2T, r_h2T[blk // 4], sc2p, sh2)

            ok = lambda t: 0 <= t < NBLK
            d_0(0)
            for i in range(NBLK + 2):
                if ok(i):
                    d_a(i)
                if i == 12:
                    ffn_load(0, extra=r_yT)
                if ok(i - 1):
                    d_b(i - 1)
                if ok(i - 2):
                    d_e(i - 2)
                if ok(i + 1):
                    d_0(i + 1)
                if ok(i - 1):
                    d_c(i - 1)
                    d_d(i - 1)
                if ok(i - 2):
                    d_f(i - 2)
            if DEBUG_STAGE == "D2" and b == 0:
                dump("accT", accT.rearrange("p c t -> p (c t)"), [P, 8 * S], F32, r_acc)
                dump("h2T", h2T.rearrange("p c t -> p (c t)"), [P, 8 * S], BF16, r_h2T)
                break

            K.mark("E%d" % b)
            K.barrier()
            for g, n in enumerate(FG_SIZES):
                if g + 1 < len(FG_SIZES):
                    ffn_load(g + 1)
                wi, wd = winb[(g + 1) % 2], wdnb[g % 2]
                rwi, rwd = r_win[(g + 1) % 2], r_wdn[g % 2]
                for f in range(n):
                    for tc in range(NTC):
                        sl = slice(tc * 512, (tc + 1) * 512)
                        psg_, rpg = PS.next()
                        for kc in range(8):
                            K.mm(psg_[:, :], wi[:, kc, f * 128:(f + 1) * 128], h2T[:, kc, sl], start=(kc == 0), stop=(kc == 7),
                                 r=[rwi, r_h2T[tc]], w=[rpg])
                        psu, rpu = PS.next()
                        for kc in range(8):
                            K.mm(psu[:, :], wi[:, kc, n * 128 + f * 128:n * 128 + (f + 1) * 128], h2T[:, kc, sl], start=(kc == 0),
                                 stop=(kc == 7), r=[rwi, r_h2T[tc]], w=[rpu])
                        si = sgi[0] % 2
                        sgi[0] += 1
                        K.actf(sgt[si], psg_[:, :], AF.Silu, r=[rpg], w=[r_sg[si]])
                        K.tt(DVE, actb[:, f, sl], psu[:, :], sgt[si], ALU.mult, r=[rpu, r_sg[si]], w=[r_actb])
                for tc in range(NTC):
                    sl = slice(tc * 512, (tc + 1) * 512)
                    for dm in range(8):
                        ps, rp = PS.next()
                        for f in range(n):
                            K.mm(ps[:, :], wd[:, f, dm * 128:(dm + 1) * 128], actb[:, f, sl], start=(f == 0), stop=(f == n - 1),
                                 r=[rwd, r_actb], w=[rp])
                        K.stt(accT[:, dm, sl], ps[:, :], g2[:, dm, b:b + 1], accT[:, dm, sl], ALU.mult, ALU.add,
                              r=[rp, r_mod, r_acc[tc]], w=[r_acc[tc]])

            K.mark("F%d" % b)
            K.barrier()
            K.dma(SP, lnrep[:, 0, :], lng_d[2], w=[r_ln])
            K.dma(SP, lnrep[:, 1, :], lng_d[3], w=[r_ln])
            st_f = {}

            def f_s1(blk):
                xt, r_xt = XT.next()
                for half in range(2):
                    ps, rp = PS.next()
                    for f in range(4):
                        dm = half * 4 + f
                        K.tr(ps[:, f * 128:(f + 1) * 128], accT[:, dm, blk * 128:(blk + 1) * 128], ident,
                             r=[r_acc[blk // 4], r_consts], w=[rp], signal=(f == 3))
                    K.copy(ACT, xt[:, half * 512:(half + 1) * 512], ps[:, :], r=[rp], w=[r_xt])
                sF, rsF = ln_stats(xt, r_xt, with_act=False)
                st_f[blk] = (xt, r_xt, sF, None, rsF)

            def f_s2a(blk):
                xt, r_xt, rstd, nmr, rs = st_f[blk]
                rstd_act(rstd, rs, EPS)
                K.actf(xt[:], xt[:], AF.Identity, bias=rstd[:, 14:15], scale=rstd[:, 15:16], r=[r_xt, rs], w=[r_xt])

            def f_s2b(blk):
                xt, r_xt, rstd, nmr, rs = st_f.pop(blk)
                K.tt(DVE, xt[:], xt[:], lnrep[:, 0, :], ALU.mult, r=[r_xt, r_ln], w=[r_xt])
                K.tt(POOL, xt[:], xt[:], lnrep[:, 1, :], ALU.add, r=[r_xt, r_ln], w=[r_xt])
                K.dma(SP, out_d[b, blk * 128:(blk + 1) * 128, :], xt[:], r=[r_xt], is_out=True)

            for t in range(NBLK + 1):
                if t < NBLK:
                    f_s1(t)
                if 0 <= t - 1 < NBLK:
                    f_s2a(t - 1)
                    f_s2b(t - 1)
            K.barrier()

        for nm, v in K.out_events.items():
            if SP.known.get(nm, 0) < v:
                nc.sync.wait_ge(K.sems[nm], v)
                SP.known[nm] = v
        K.mark("end")
        build_program.n_ins = K.n_ins
        build_program.marks = K.marks
    return nc


def _grp(w, col0, ncols, gw):
    K_ = w.shape[0]
    sub = w[:, col0:col0 + ncols].reshape(K_ // 128, 128, ncols // gw, gw)
    return np.ascontiguousarray(sub.transpose(2, 1, 0, 3).reshape(ncols // gw, 128, (K_ // 128) * gw))


def _colT(v):
    return np.ascontiguousarray(v.reshape(-1, 128).T)


def _rep(v):
    return np.ascontiguousarray(np.broadcast_to(v[None, :], (128, v.shape[0])))


def prepare_shared(inp):
    f = lambda k: np.asarray(inp[k], dtype=np.float32)[0]
    w_in, b_in = f("w_in"), f("b_in")
    sh = {}
    sh["wada"] = _grp(f("w_ada"), 0, 6144, 128)
    sh["badaT"] = _colT(f("b_ada"))
    sh["w128"] = np.concatenate([_grp(w_in, 0, 512, 128), _grp(w_in, 512, 512, 128), _grp(w_in, 1024, 512, 128),
                                 _grp(w_in, 1544, 1024, 128)], axis=0)
    sh["w256"] = np.concatenate([_grp(w_in, 2568, 1024, 256), _grp(w_in, 3600, 1024, 256)], axis=0)
    gcols = np.concatenate([w_in[:, 1536:1544], w_in[:, 3592:3600]], axis=1)
    sh["wgate"] = _grp(gcols, 0, 16, 16)[0]
    wq, wk = f("w_mq"), f("w_mk")
    sh["wqk"] = np.stack([np.concatenate([_grp(wq[h], 0, 128, 128)[0], _grp(wk[h], 0, 128, 128)[0]], axis=1)
                          for h in range(4)], axis=0)
    wpa, wpb = f("w_pa"), f("w_pb")
    sh["wd1a"] = np.concatenate([_grp(wpa, 0, 1024, 128), _grp(wpb, 0, 1024, 128)], axis=2)
    sh["wd1b"] = np.concatenate([_grp(w_in, 4624, 1024, 128), _grp(w_in, 5648, 1024, 128)], axis=2)
    sh["wout"] = _grp(f("w_out"), 0, 1024, 1024)[0]
    wfi, wfd = f("w_ffn_in"), f("w_ffn_down")
    c0 = 0
    for g, n in enumerate(FG_SIZES):
        gate = wfi[:, c0 * 128:(c0 + n) * 128].reshape(8, 128, n * 128)
        up = wfi[:, DFF + c0 * 128:DFF + (c0 + n) * 128].reshape(8, 128, n * 128)
        both = np.concatenate([gate, up], axis=2)
        sh["wfin%d" % g] = np.ascontiguousarray(both.transpose(1, 0, 2).reshape(128, 8 * 2 * n * 128))
        dn = wfd[c0 * 128:(c0 + n) * 128, :].reshape(n, 128, D)
        sh["wfdn%d" % g] = np.ascontiguousarray(dn.transpose(1, 0, 2).reshape(128, n * D))
        c0 += n
    sh["bfm"] = np.concatenate([_colT(b_in[0:512]), _colT(b_in[512:1024]), _colT(b_in[1544:2568]),
                                _colT(b_in[4624:5648]), _colT(b_in[5648:6672]), _colT(b_in[3600:4624])], axis=1)
    bg = np.concatenate([b_in[1536:1544], b_in[3592:3600]])
    sh["bgate"] = _rep(np.tile(bg, 16))
    sh["btok"] = _rep(np.concatenate([b_in[1024:1536], b_in[2568:3592], b_in[3600:4624]]))
    sh["convw"] = np.ascontiguousarray(f("conv_w").reshape(4, 8, 128).transpose(2, 0, 1).reshape(128, 32))
    sh["convb"] = _colT(f("conv_b"))
    sh["mhg"] = _colT(f("mh_norm_g"))
    sh["lng"] = np.stack([_rep(f("ln1_g")), _rep(f("ln1_b")), _rep(f("ln2_g")), _rep(f("ln2_b"))], axis=0)
    eye = np.eye(128, dtype=np.float32)
    tri = np.triu(np.ones((128, 128), np.float32))
    ones = np.ones((128, 128), np.float32)
    mneg = np.where(tri > 0, 0.0, -30000.0).astype(np.float32)
    sh["consts"] = np.ascontiguousarray(np.concatenate([eye, tri, ones, mneg, tri], axis=1))
    return {k: np.ascontiguousarray(v, dtype=np.float32) for k, v in sh.items()}


def core_inputs(inp, shared, core):
    x = np.asarray(inp["x"], dtype=np.float32)
    c = np.asarray(inp["c"], dtype=np.float32)
    m = dict(shared)
    m["x"] = np.ascontiguousarray(x[core * NB:(core + 1) * NB])
    cl = c[core * NB:(core + 1) * NB]
    m["cT"] = np.ascontiguousarray(cl.reshape(NB, 8, 128).transpose(2, 1, 0).reshape(128, 16))
    return m


def kernel(**inputs):
    n = 8
    shared = prepare_shared(inputs)
    nc = build_program()
    in_maps = [core_inputs(inputs, shared, i) for i in range(n)]
    res = run_bass_kernel_spmd(nc, in_maps, core_ids=list(range(n)))
    out = np.concatenate([np.asarray(r["out"]) for r in res.results], axis=0)
    return out.astype(np.float32, copy=False)
```

```python
import numpy as np
from contextlib import ExitStack
import concourse.bass as bass
import concourse.mybir as mybir
from concourse.bass_utils import run_bass_kernel_spmd

F32 = mybir.dt.float32
BF16 = mybir.dt.bfloat16
AF = mybir.ActivationFunctionType
ALU = mybir.AluOpType

P = 128
S = 2048
D = 1024
NB = 2
NBLK = S // P
NTC = 4
DFF = 2816
NFC = DFF // P
FG_SIZES = [4, 4, 4, 4, 4, 2]
ALPHA = 2.0 ** 0.25
EPS = 1e-5
KD = 4

DEBUG_STAGE = None
_DUMPS = []


class Reg:
    __slots__ = ("name", "w", "r")

    def __init__(self, name):
        self.name = name
        self.w = {}
        self.r = {}


class Eng:
    def __init__(self, kb, name, h):
        self.name = name
        self.h = h
        self.semname = "s_" + name
        self.sem = kb.new_sem(self.semname)
        self.seq = 0
        self.known = {}
        self.dma_i = 0
        self.dsem = []


class KB:
    def __init__(self, nc, es):
        self.nc = nc
        self.es = es
        self.sems = {}
        self.pe = Eng(self, "pe", nc.tensor)
        self.act = Eng(self, "act", nc.scalar)
        self.dve = Eng(self, "dve", nc.vector)
        self.pool = Eng(self, "pool", nc.gpsimd)
        self.sp = Eng(self, "sp", nc.sync)
        self.engs = [self.pe, self.act, self.dve, self.pool, self.sp]
        for q in (self.sp, self.pool, self.act):
            for i in range(KD):
                nm = "d_%s%d" % (q.name, i)
                self.new_sem(nm)
                q.dsem.append(nm)
        self.out_events = {}
        self.n_ins = 0
        self.pe_n = 0
        self.marks = []

    def new_sem(self, name):
        s = self.es.enter_context(self.nc.semaphore(name))
        self.sems[name] = s
        return s

    def _deps(self, eng, r, w, waw=True):
        deps = {}
        for x in r:
            for k, v in x.w.items():
                if deps.get(k, 0) < v:
                    deps[k] = v
        for x in w:
            if waw:
                for k, v in x.w.items():
                    if deps.get(k, 0) < v:
                        deps[k] = v
            for k, v in x.r.items():
                if deps.get(k, 0) < v:
                    deps[k] = v
        for k, v in deps.items():
            if eng is self.pe and k == self.pe.semname:
                continue
            if eng.known.get(k, 0) < v:
                eng.h.wait_ge(self.sems[k], v)
                eng.known[k] = v
                self.n_ins += 1

    def op(self, eng, fn, r=(), w=(), signal=True, waw=True):
        self._deps(eng, r, w, waw)
        ins = fn()
        self.n_ins += 1
        if eng is self.pe:
            self.pe_n += 1
        if signal:
            eng.seq += 1
            ins.then_inc(eng.sem, 1)
            ev = eng.seq
        else:
            ev = eng.seq + 1
        k = eng.semname
        for x in w:
            if x.w.get(k, 0) < ev:
                x.w[k] = ev
        for x in r:
            if x.r.get(k, 0) < ev:
                x.r[k] = ev
        return ins

    def dma(self, q, out, in_, r=(), w=(), is_out=False, **kw):
        i = q.dma_i
        nm = q.dsem[i % KD]
        tgt = 16 * (i // KD + 1)
        prev = tgt - 16
        if prev > 0 and q.known.get(nm, 0) < prev:
            q.h.wait_ge(self.sems[nm], prev)
            q.known[nm] = prev
        self._deps(q, r, w)
        q.h.dma_start(out=out, in_=in_, **kw).then_inc(self.sems[nm], 16)
        self.n_ins += 1
        q.dma_i += 1
        for x in w:
            x.w[nm] = tgt
        for x in r:
            x.r[nm] = tgt
        if is_out:
            self.out_events[nm] = tgt

    def mark(self, label):
        self.marks.append((label, self.pe_n))

    def barrier(self):
        ev = {}
        for e in self.engs:
            if e.seq > 0:
                ev[e.semname] = e.seq
            for j, nm in enumerate(e.dsem):
                n = (e.dma_i - j + KD - 1) // KD if e.dma_i > j else 0
                if n > 0:
                    ev[nm] = 16 * n
        for e in self.engs:
            for k, v in ev.items():
                if e.known.get(k, 0) < v:
                    e.h.wait_ge(self.sems[k], v)
                    e.known[k] = v
                    self.n_ins += 1

    def mm(self, out, lhsT, rhs, start, stop, r=(), w=(), signal=None, sgc=False):
        if signal is None:
            signal = stop
        return self.op(self.pe, lambda: self.nc.tensor.matmul(out, lhsT, rhs, start=start, stop=stop,
                                                              skip_group_check=sgc),
                       r=r, w=w, signal=signal)

    def tr(self, out, in_, ident, r=(), w=(), signal=True):
        return self.op(self.pe, lambda: self.nc.tensor.transpose(out, in_, ident), r=r, w=w, signal=signal)

    def actf(self, out, in_, func, bias=None, scale=None, r=(), w=(), waw=True):
        kw = {}
        if bias is not None:
            kw["bias"] = bias
        if scale is not None:
            kw["scale"] = scale
        return self.op(self.act, lambda: self.nc.scalar.activation(out=out, in_=in_, func=func, **kw), r=r, w=w,
                       waw=waw)

    def tt(self, eng, out, in0, in1, op, r=(), w=()):
        return self.op(eng, lambda: eng.h.tensor_tensor(out=out, in0=in0, in1=in1, op=op), r=r, w=w)

    def ts(self, eng, out, in0, s1, s2, op0, op1=None, r=(), w=(), waw=True):
        if op1 is None:
            return self.op(eng, lambda: eng.h.tensor_scalar(out=out, in0=in0, scalar1=s1, scalar2=None, op0=op0),
                           r=r, w=w, waw=waw)
        return self.op(eng, lambda: eng.h.tensor_scalar(out=out, in0=in0, scalar1=s1, scalar2=s2, op0=op0, op1=op1),
                       r=r, w=w, waw=waw)

    def stt(self, out, in0, scalar, in1, op0, op1, r=(), w=()):
        return self.op(self.dve, lambda: self.nc.vector.scalar_tensor_tensor(
            out=out, in0=in0, scalar=scalar, in1=in1, op0=op0, op1=op1), r=r, w=w)

    def copy(self, eng, out, in_, r=(), w=()):
        if eng is self.act:
            return self.op(eng, lambda: self.nc.scalar.activation(out=out, in_=in_, func=AF.Copy), r=r, w=w)
        return self.op(eng, lambda: eng.h.tensor_copy(out=out, in_=in_), r=r, w=w)

    def memset(self, eng, ap, val, w=()):
        return self.op(eng, lambda: eng.h.memset(ap, val), w=w)


class Pool:
    def __init__(self, items):
        self.items = items
        self.i = 0

    def next(self):
        it = self.items[self.i % len(self.items)]
        self.i += 1
        return it


def build_program():
    nc = bass.Bass("TRN2", target_bir_lowering=False)
    dt = nc.dram_tensor

    def din(name, shape):
        return dt(name, list(shape), F32, kind="ExternalInput").ap()

    x_d = din("x", [NB, S, D])
    cT_d = din("cT", [P, 16])
    wada_d = din("wada", [48, P, 1024])
    badaT_d = din("badaT", [P, 48])
    w128_d = din("w128", [20, P, 1024])
    w256_d = din("w256", [8, P, 2048])
    wgate_d = din("wgate", [P, 128])
    wqk_d = din("wqk", [4, P, 512])
    wd1a_d = din("wd1a", [8, P, 1536])
    wd1b_d = din("wd1b", [8, P, 2048])
    wout_d = din("wout", [P, 8192])
    wfin_d = [din("wfin%d" % g, [P, 8 * 2 * n * P]) for g, n in enumerate(FG_SIZES)]
    wfdn_d = [din("wfdn%d" % g, [P, n * D]) for g, n in enumerate(FG_SIZES)]
    bfm_d = din("bfm", [P, 40])
    bgate_d = din("bgate", [P, 256])
    btok_d = din("btok", [P, 2560])
    convw_d = din("convw", [P, 32])
    convb_d = din("convb", [P, 8])
    mhg_d = din("mhg", [P, 8])
    lng_d = din("lng", [4, P, D])
    consts_d = din("consts", [P, 640])
    out_d = dt("out", [NB, S, D], F32, kind="ExternalOutput").ap()

    with ExitStack() as es:
        K = KB(nc, es)
        PE, ACT, DVE, POOL, SP = K.pe, K.act, K.dve, K.pool, K.sp

        def sb(name, shape, dtype=F32):
            return es.enter_context(nc.sbuf_tensor("sb_" + name, list(shape), dtype))

        ARENA_W = 152 * 256
        arena = sb("arena", [P, ARENA_W])

        def av(off_b, nbytes, dtype, pat=None, **kw):
            a = arena[:, off_b // 4:(off_b + nbytes) // 4]
            if dtype is BF16:
                a = a.bitcast(BF16)
            if pat is not None:
                a = a.rearrange(pat, **kw)
            return a

        KB_ = 1024
        hT = av(0, 32 * KB_, BF16, "p (c t) -> p c t", c=8)
        attT = av(32 * KB_, 16 * KB_, BF16, "p (c t) -> p c t", c=4)
        hmT = av(48 * KB_, 32 * KB_, BF16, "p (c t) -> p c t", c=8)
        TR = 80 * KB_
        yT = av(120 * KB_, 32 * KB_, BF16, "p (c t) -> p c t", c=8)
        accT = av(32 * KB_, 64 * KB_, F32, "p (c t) -> p c t", c=8)
        lnrep = av(112 * KB_, 8 * KB_, F32, "p (c t) -> p c t", c=2)
        actb = av(96 * KB_, 16 * KB_, BF16, "p (c t) -> p c t", c=4)
        o = TR
        QT = av(o, 4096, BF16); o += 4096
        KTh = [av(o, 4096, BF16), av(o + 4096, 4096, BF16)]; o += 8192
        VX = av(o, 4224, BF16, "p (j h e) -> p j h e", j=16, h=2); o += 4224
        PTs = [av(o + i * 1024, 1024, BF16) for i in range(4)]; o += 4096
        biasTab = av(o, 8192, F32, "p (h i j) -> p h i j", h=8, i=16); o += 8192
        bfv = av(o, 2048, F32); o += 2048
        attok = av(o, 1024, BF16, "p (i e) -> p i e", i=4); o += 1024
        o = TR
        uT = av(o, 8192, BF16, "p (c t) -> p c t", c=2); o += 8192
        qT = av(o, 4096, BF16); o += 4096
        kT = av(o, 4096, BF16); o += 4096
        khat = av(o, 4096, BF16, "p (j e) -> p j e", j=16); o += 4096
        vext = av(o, 8256, BF16, "p (j e) -> p j e", j=16); o += 8256
        Sb = av(o, 8256, BF16, "p (j e) -> p j e", j=16); o += 8256
        Sf = [av(o + i * 1032, 1028, F32) for i in range(2)]; o += 2064
        muT = [av(o + i * 4112, 4104, BF16) for i in range(2)]; o += 8224
        Dg = av(o, 2048, BF16, "p (c j e) -> p c j e", c=2, j=4); o += 2048
        bmvo = av(o, 4096, F32); o += 4096
        mhg = av(o, 32, F32); o += 32
        cw = [av(o + i * 1024, 1024, F32) for i in range(6)]; o += 6144
        cwb = [av(o + i * 512, 512, BF16) for i in range(2)]; o += 1024
        sdtb = [av(o + i * 256, 256, BF16) for i in range(2)]; o += 512
        assert o <= 152 * KB_, o
        d1t = [av(TR + i * 2048, 2048, F32) for i in range(8)]
        winb = [av(112 * KB_ + i * 16 * KB_, 16 * KB_, BF16, "p (k n) -> p k n", k=8) for i in range(2)]

        wpool_t = sb("wpool", [P, 4, 2048], BF16)
        wdnb = [av(144 * KB_, 8 * KB_, BF16, "p (f n) -> p f n", f=4),
                wpool_t[:, 0:2, :].rearrange("p a (b n) -> p (a b) n", b=2)]
        wout_v = av(96 * KB_, 16 * KB_, BF16)
        yog = wpool_t[:, :, :].rearrange("p a n -> p (a n)").bitcast(F32).rearrange("p (c t) -> p c t", c=8)

        consts = sb("consts", [P, 640])
        ident = consts[:, 0:128]
        triU = consts[:, 128:256]
        ones = consts[:, 256:384]
        mask01 = consts[:, 512:640]
        cbf = sb("cbf", [P, 640], BF16)
        identb = cbf[:, 0:128]
        mnegb = cbf[:, 128:256]
        zerob = cbf[:, 256:640]
        xts = [sb("xt%d" % i, [P, D]) for i in range(3)]
        xns = [sb("xn%d" % i, [P, D]) for i in range(2)]
        small = sb("small", [P, 64, 16])
        cTt = sb("cTt", [P, 16])
        siluT = sb("siluT", [P, 8, 2], BF16)
        badaT = sb("badaT", [P, 48])
        modT = sb("modT", [P, 48, 2])
        sc1p = sb("sc1p", [P, 8, 2])
        sc2p = sb("sc2p", [P, 8, 2])
        bfm = sb("bfm", [P, 40])
        bq8 = sb("bq8", [P, 4])
        bgate = sb("bgate", [P, 256])
        convw = sb("convw", [P, 32])
        convb = sb("convb", [P, 8])
        wg = sb("wg", [P, 128], BF16)
        gl = sb("gl", [P, 16, 16])
        nl = sb("nl", [P, 16, 16])
        nlf = sb("nlf", [P, 16, 8])
        nlm = sb("nlm", [P, 16, 4])
        igm = sb("igm", [P, 16, 4])
        Gt = sb("Gt", [P, 16, 8])
        Gincl = sb("Gincl", [P, 16, 8])
        Gf = sb("Gf", [P, 16, 8])
        colfac = sb("colfac", [P, 16, 4])
        ahat = sb("ahat", [P, 16, 4])
        enegb = sb("enegb", [P, 16, 4])
        EGt = sb("EGt", [P, 16, 4])
        enegb2 = sb("enegb2", [P, 16, 4])
        gtmp = sb("gtmp", [P, 16, 4])
        rec4 = sb("rec4", [P, 8])

        psum = [es.enter_context(nc.psum_tensor("ps%d" % i, [P, 512], F32)) for i in range(8)]
        _bank = [(psum[i], Reg("ps%d" % i)) for i in range(8)]
        PS = Pool(_bank[2:8])
        PSA = Pool(_bank[0:2])
        PSO = Pool(_bank[0:3])
        PSC = Pool(_bank[3:8])

        R = lambda n: Reg(n)
        r_consts, r_cbf = R("consts"), R("cbf")
        r_hT = [R("hT%d" % i) for i in range(NTC)]
        r_attT = [R("attT%d" % i) for i in range(NTC)]
        r_hmT = [R("hmT%d" % i) for i in range(NTC)]
        r_yT = [R("yT%d" % i) for i in range(NTC)]
        r_acc = [R("acc%d" % i) for i in range(NTC)]
        r_h2T = r_hT
        r_mod = R("mod")
        r_misc = R("misc")
        r_gates = R("gates")
        XT = Pool([(xts[i], R("xt%d" % i)) for i in range(3)])
        XN = Pool([(xns[i], R("xn%d" % i)) for i in range(2)])
        WP = Pool([(wpool_t[:, i, :], R("wp%d" % i)) for i in range(4)])
        PTP = Pool([(PTs[i], R("pt%d" % i)) for i in range(4)])
        small_i = [0]

        def sm():
            i = small_i[0] % 64
            small_i[0] += 1
            return small[:, i, :]
        r_small = [R("sm%d" % i) for i in range(64)]

        def smr():
            i = small_i[0] % 64
            small_i[0] += 1
            return small[:, i, :], r_small[i]

        class WPre:
            def __init__(self):
                self.plan, self.pos, self.issued = [], 0, []

            def set_plan(self, items):
                assert not self.issued and self.pos == len(self.plan)
                self.plan, self.pos = items, 0
                self.top_up()

            def top_up(self, n=3):
                while len(self.issued) < n and self.pos < len(self.plan):
                    cols, src = self.plan[self.pos]
                    wt, rw = WP.next()
                    wload(wt[:, 0:cols], src, [rw])
                    self.issued.append((wt, rw))
                    self.pos += 1

            def get(self):
                assert self.issued
                return self.issued.pop(0)

        wdma_q = [0]

        def wload(dst, src, w, **kw):
            K.dma(POOL, dst, src, w=w, max_dma_last_dim=4096, **kw)

        WPF = WPre()

        def dump(name, ap, shape, dtype, r):
            t = dt("dbg_" + name, list(shape), dtype, kind="ExternalOutput").ap()
            K.dma(SP, t, ap, r=r, is_out=True)
            _DUMPS.append("dbg_" + name)

        def ln_stats(src, r_src, width=D, with_act=True):
            s, rs = smr()
            nch = width // 512 if width >= 512 else 1
            cw_ = width // nch
            for c in range(nch):
                K.op(DVE, lambda c=c: nc.vector.bn_stats(out=s[:, c * 6:(c + 1) * 6], in_=src[:, c * cw_:(c + 1) * cw_]),
                     r=[r_src], w=[rs])
            K.op(DVE, lambda: nc.vector.bn_aggr(out=s[:, 12:14], in_=s[:, 0:6 * nch]), r=[rs], w=[rs])
            if not with_act:
                return s, rs
            rstd_act(s, rs, EPS)
            return s[:, 15:16], s[:, 14:15], rs

        def rstd_act(s, rs, bias, in_ap=None, scale=1.0):
            K.actf(s[:, 0:1], s[:, 13:14] if in_ap is None else in_ap, AF.Ln, bias=bias, scale=scale, r=[rs], w=[rs])
            K.actf(s[:, 15:16], s[:, 0:1], AF.Exp, scale=-0.5, r=[rs], w=[rs])
            K.op(ACT, lambda: nc.scalar.mul(out=s[:, 1:2], in_=s[:, 15:16], mul=-1.0), r=[rs], w=[rs])
            K.actf(s[:, 14:15], s[:, 12:13], AF.Identity, scale=s[:, 1:2], r=[rs], w=[rs])

        K.dma(SP, consts[:], consts_d[:, :], w=[r_consts])
        K.dma(SP, cTt[:], cT_d[:, :], w=[r_misc])
        K.dma(SP, badaT[:], badaT_d[:, :], w=[r_misc])
        K.dma(SP, bfm[:], bfm_d[:, :], w=[r_misc])
        K.dma(SP, bgate[:], bgate_d[:, :], w=[r_misc])
        K.dma(SP, convw[:], convw_d[:, :], w=[r_misc])
        K.dma(SP, convb[:], convb_d[:, :], w=[r_misc])
        r_wg = R("wg")
        wload(wg[:], wgate_d[:, :], [r_wg])
        K.copy(DVE, identb, ident, r=[r_consts], w=[r_cbf])
        K.copy(DVE, mnegb, consts[:, 384:512], r=[r_consts], w=[r_cbf])
        K.memset(DVE, zerob, 0.0, w=[r_cbf])
        K.ts(DVE, bq8[:], bfm[:, 0:4], 0.125, None, ALU.mult, r=[r_misc], w=[r_misc])
        K.actf(siluT[:].rearrange("p a b -> p (a b)"), cTt[:], AF.Silu, r=[r_misc], w=[r_misc])

        psm, r_psm = PS.next()
        WPF.set_plan([(1024, wada_d[g]) for g in range(48)])
        for g in range(48):
            wt, rw = WPF.get()
            for kc in range(8):
                K.mm(psm[:, g * 2:g * 2 + 2], wt[:, kc * 128:(kc + 1) * 128], siluT[:, kc, :],
                     start=(kc == 0), stop=(kc == 7), r=[rw, r_misc], w=[r_psm])
            WPF.top_up()
        psm3 = psm[:, 0:96].rearrange("p (g b) -> p g b", b=2)
        for b in range(2):
            K.tt(DVE, modT[:, :, b], psm3[:, :, b], badaT[:], ALU.add, r=[r_psm, r_misc], w=[r_mod])
        K.ts(DVE, sc1p[:], modT[:, 8:16, :], 1.0, None, ALU.add, r=[r_mod], w=[r_mod])
        K.ts(DVE, sc2p[:], modT[:, 32:40, :], 1.0, None, ALU.add, r=[r_mod], w=[r_mod])
        sh1 = modT[:, 0:8, :]
        g1 = modT[:, 16:24, :]
        sh2 = modT[:, 24:32, :]
        g2 = modT[:, 40:48, :]
        if DEBUG_STAGE == "mod":
            dump("mod", modT[:].rearrange("p a b -> p (a b)"), [P, 96], F32, [r_mod])

        evac_i = [0]

        def evac_eng():
            evac_i[0] += 1
            return ACT if evac_i[0] % 2 else DVE

        def affine_evac(out, in_, scale_ap, bias_ap, r, w, eng=None):
            e = eng if eng is not None else evac_eng()
            if e is ACT:
                K.actf(out, in_, AF.Identity, bias=bias_ap, scale=scale_ap, r=r, w=w, waw=True)
            else:
                K.ts(DVE, out, in_, scale_ap, bias_ap, ALU.mult, ALU.add, r=r, w=w, waw=True)

        def ln_norm(src, r_src, rstd, nmr, rs):
            xn, r_xn = XN.next()
            K.actf(xn[:], src[:], AF.Identity, bias=nmr, scale=rstd, r=[r_src, rs], w=[r_xn])
            return xn, r_xn

        def ln_T(b, blk, xn, r_xn, dstT, r_dst, scp, shf, all_act=False):
            for half in range(2):
                ps, rp = PS.next()
                for f in range(4):
                    fc = half * 4 + f
                    K.tr(ps[:, f * 128:(f + 1) * 128], xn[:, fc * 128:(fc + 1) * 128], ident,
                         r=[r_xn, r_consts], w=[rp], signal=(f == 3))
                for f in range(4):
                    fc = half * 4 + f
                    affine_evac(dstT[:, fc, blk * 128:(blk + 1) * 128], ps[:, f * 128:(f + 1) * 128],
                                scp[:, fc, b:b + 1], shf[:, fc, b:b + 1], r=[rp, r_mod], w=[r_dst],
                                eng=(ACT if (half == 0 or all_act) else DVE))

        for b in range(NB):
            plan = []
            for hp_ in range(4):
                plan += [(1024, w128_d[hp_]), (1024, w128_d[4 + hp_]), (1024, w128_d[8 + hp_])]
            for h_ in range(4):
                plan += [(1024, w128_d[12 + 2 * h_]), (1024, w128_d[13 + 2 * h_]), (512, wqk_d[h_]),
                         (2048, w256_d[h_]), (2048, w256_d[4 + h_])]
            for dm_ in range(8):
                plan += [(1536, wd1a_d[dm_]), (2048, wd1b_d[dm_])]
            WPF.set_plan(plan)
            K.mark("A%d" % b)
            st_a = {}

            xld = {}

            def a_0(blk):
                xt, r_xt = XT.next()
                K.dma(SP, xt[:], x_d[b, blk * 128:(blk + 1) * 128, :], w=[r_xt])
                xld[blk] = (xt, r_xt)

            def a_1(blk):
                xt, r_xt = xld.pop(blk)
                sA, rsA = ln_stats(xt, r_xt, with_act=False)
                st_a[blk] = (xt, r_xt, sA, rsA)

            def a_2(blk):
                xt, r_xt, sA, rsA = st_a[blk]
                rstd_act(sA, rsA, EPS)
                st_a[blk] = ln_norm(xt, r_xt, sA[:, 15:16], sA[:, 14:15], rsA)

            def a_3(blk):
                xn, r_xn = st_a.pop(blk)
                ln_T(b, blk, xn, r_xn, hT, r_hT[blk // 4], sc1p, sh1)

            a_0(0)
            for t in range(NBLK + 2):
                if 0 <= t - 1 < NBLK:
                    a_2(t - 1)
                if t + 1 < NBLK:
                    a_0(t + 1)
                if t < NBLK:
                    a_1(t)
                if 0 <= t - 2 < NBLK:
                    a_3(t - 2)
            if DEBUG_STAGE == "A" and b == 0:
                dump("hT", hT.rearrange("p c t -> p (c t)"), [P, 8 * S], BF16, r_hT)
                break

            K.mark("B0%d" % b)
            psg, r_psg = PS.next()
            for j in range(NBLK):
                for kc in range(8):
                    K.mm(psg[:, j * 16:(j + 1) * 16], hT[:, kc, j * 128:(j + 1) * 128], wg[:, kc * 16:(kc + 1) * 16],
                         start=(kc == 0), stop=(kc == 7), r=[r_hT[j // 4], r_wg], w=[r_psg])
            glf = gl[:].rearrange("p a b -> p (a b)")
            nlf_ = nl[:].rearrange("p a b -> p (a b)")
            K.tt(DVE, glf, psg[:, 0:256], bgate[:], ALU.add, r=[r_psg, r_misc], w=[r_gates])
            K.actf(nlf_, glf, AF.Exp, scale=-1.0, r=[r_gates], w=[r_gates])
            K.actf(nlf_, nlf_, AF.Ln, bias=1.0, scale=1.0, r=[r_gates], w=[r_gates])
            K.copy(DVE, nlf[:], nl[:, :, 0:8], r=[r_gates], w=[r_gates])
            K.copy(DVE, nlm[:], nl[:, :, 12:16], r=[r_gates], w=[r_gates])
            K.copy(DVE, igm[:], gl[:, :, 8:12], r=[r_gates], w=[r_gates])
            psw, r_psw = PS.next()
            K.mm(psw[:, 0:128], triU, nlf[:].rearrange("p a b -> p (a b)"), True, True, r=[r_gates, r_consts], w=[r_psw])
            K.mm(psw[:, 128:256], ones, nlf[:].rearrange("p a b -> p (a b)"), True, True, r=[r_gates, r_consts], w=[r_psw])
            K.mm(psw[:, 256:320], triU, nlm[:].rearrange("p a b -> p (a b)"), True, True, r=[r_gates, r_consts], w=[r_psw])
            K.mm(psw[:, 320:384], ones, nlm[:].rearrange("p a b -> p (a b)"), True, True, r=[r_gates, r_consts], w=[r_psw])
            Gtf = Gt[:].rearrange("p a b -> p (a b)")
            K.copy(DVE, Gtf, psw[:, 128:256], r=[r_psw], w=[r_gates])
            K.copy(DVE, Gincl[:, 0, :], Gt[:, 0, :], r=[r_gates], w=[r_gates])
            for j in range(1, NBLK):
                K.tt(DVE, Gincl[:, j, :], Gincl[:, j - 1, :], Gt[:, j, :], ALU.add, r=[r_gates], w=[r_gates])
            Gff = Gf[:].rearrange("p a b -> p (a b)")
            K.tt(DVE, Gff, psw[:, 0:128], Gincl[:].rearrange("p a b -> p (a b)"), ALU.add, r=[r_psw, r_gates], w=[r_gates])
            K.tt(DVE, Gff, Gff, Gtf, ALU.subtract, r=[r_gates], w=[r_gates])
            r_bt = R("biasTab")
            for h in range(8):
                for qc_ in range(NTC):
                    i = 4 * qc_ + 3
                    K.ts(DVE, biasTab[:, h, qc_, 0:i + 1], Gf[:, 0:i + 1, h], Gincl[:, i, h:h + 1], None, ALU.subtract,
                         r=[r_gates], w=[r_bt])
            gt_ = gtmp[:].rearrange("p a b -> p (a b)")
            K.tt(DVE, gt_, psw[:, 256:320], igm[:].rearrange("p a b -> p (a b)"), ALU.add, r=[r_psw, r_gates], w=[r_gates])
            K.actf(colfac[:].rearrange("p a b -> p (a b)"), gt_, AF.Exp, r=[r_gates], w=[r_gates])
            K.actf(enegb[:].rearrange("p a b -> p (a b)"), psw[:, 256:320], AF.Exp, r=[r_psw], w=[r_gates])
            K.actf(EGt[:].rearrange("p a b -> p (a b)"), psw[:, 320:384], AF.Exp, scale=-1.0, r=[r_psw], w=[r_gates])
            K.tt(DVE, enegb2[:].rearrange("p a b -> p (a b)"), enegb[:].rearrange("p a b -> p (a b)"),
                 enegb[:].rearrange("p a b -> p (a b)"), ALU.mult, r=[r_gates], w=[r_gates])
            K.tt(DVE, gt_, gt_, psw[:, 320:384], ALU.subtract, r=[r_psw, r_gates], w=[r_gates])
            K.actf(ahat[:].rearrange("p a b -> p (a b)"), gt_, AF.Exp, r=[r_gates], w=[r_gates])
            K.dma(SP, bfv, btok_d[:, 0:512], w=[r_bt])
            if DEBUG_STAGE == "B0" and b == 0:
                dump("G", Gff, [P, 128], F32, [r_gates])
                dump("colfac", colfac[:].rearrange("p a b -> p (a b)"), [P, 64], F32, [r_gates])
                dump("ahat", ahat[:].rearrange("p a b -> p (a b)"), [P, 64], F32, [r_gates])
                dump("EG", EGt[:].rearrange("p a b -> p (a b)"), [P, 64], F32, [r_gates])
                break

            K.mark("B%d" % b)
            r_QT, r_KT, r_VX, r_attok = R("QT"), R("KT"), R("VX"), R("attok")
            K.memset(DVE, VX[:, :, :, 64:65], 1.0, w=[r_VX])
            K.memset(DVE, KTh[0], 0.0, w=[r_KT])
            K.memset(POOL, KTh[1], 0.0, w=[r_KT])
            for hp in range(4):
                wq, rwq = WPF.get()
                wk, rwk = WPF.get()
                wv, rwv = WPF.get()
                for tc in range(NTC):
                    ps, rp = PS.next()
                    for kc in range(8):
                        K.mm(ps[:, :], wq[:, kc * 128:(kc + 1) * 128], hT[:, kc, tc * 512:(tc + 1) * 512],
                             start=(kc == 0), stop=(kc == 7), r=[rwq, r_hT[tc]], w=[rp])
                    K.actf(QT[:, tc * 512:(tc + 1) * 512], ps[:, :], AF.Identity, bias=bq8[:, hp:hp + 1], scale=0.125,
                           r=[rp, r_misc], w=[r_QT])
                    ps, rp = PS.next()
                    for kc in range(8):
                        K.mm(ps[:, :], wk[:, kc * 128:(kc + 1) * 128], hT[:, kc, tc * 512:(tc + 1) * 512],
                             start=(kc == 0), stop=(kc == 7), r=[rwk, r_hT[tc]], w=[rp])
                    for hl_ in range(2):
                        pr = slice(hl_ * 64, (hl_ + 1) * 64)
                        K.ts(DVE, KTh[hl_][pr, tc * 512:(tc + 1) * 512], ps[pr, :], bfm[pr, 4 + hp:5 + hp], None, ALU.add,
                             r=[rp, r_misc], w=[r_KT])
                for j in range(NBLK):
                    ps, rp = PS.next()
                    jl = 0
                    for kc in range(8):
                        K.mm(ps[:, jl * 128:(jl + 1) * 128], hT[:, kc, j * 128:(j + 1) * 128], wv[:, kc * 128:(kc + 1) * 128],
                             start=(kc == 0), stop=(kc == 7), r=[rwv, r_hT[j // 4]], w=[rp])
                    K.tt(DVE, VX[:, j, :, 0:64], ps[:, jl * 128:(jl + 1) * 128].rearrange("p (h e) -> p h e", h=2),
                         bfv[:, hp * 128:(hp + 1) * 128].rearrange("p (h e) -> p h e", h=2), ALU.add,
                         r=[rp, r_bt], w=[r_VX])
                WPF.top_up()
                for qc in range(NTC):
                    for hl in range(2):
                        h = hp * 2 + hl
                        pv, r_pv = PSA.next()
                        K.mm(pv[:, 0:264], zerob[0:1, 0:128], zerob[0:1, 0:264], True, False, r=[r_cbf], w=[r_pv],
                             signal=True, sgc=True)
                        nst = 4 * qc + 4
                        sts = {}

                        def emitS(j):
                            a = max(0, j - 4 * qc)
                            st, r_st = PS.next()
                            diag = j >= 4 * qc
                            K.mm(st[:, a * 128:512], KTh[hl][:, j * 128:(j + 1) * 128],
                                 QT[:, qc * 512 + a * 128:(qc + 1) * 512],
                                 start=True, stop=(not diag), r=[r_KT, r_QT], w=[r_st])
                            if diag:
                                K.mm(st[:, a * 128:(a + 1) * 128], identb, mnegb, start=False, stop=True,
                                     r=[r_cbf], w=[r_st])
                            sts[j] = (st, r_st, a)

                        def emitEPV(j):
                            st, r_st, a = sts.pop(j)
                            pt, r_pt = PTP.next()
                            K.actf(pt[:, a * 128:512], st[:, a * 128:512], AF.Exp,
                                   bias=biasTab[:, h, qc, j:j + 1], scale=1.0, r=[r_st, r_bt], w=[r_pt])
                            for il in range(a, 4):
                                i = 4 * qc + il
                                K.mm(pv[:, il * 66:il * 66 + 65], pt[:, il * 128:(il + 1) * 128], VX[:, j, hl, 0:65],
                                     start=False, stop=(j == nst - 1 and il == 3), r=[r_pt, r_VX], w=[r_pv],
                                     signal=(il == 3), sgc=True)

                        emitS(0)
                        if nst > 1:
                            emitS(1)
                        for j in range(nst):
                            if j + 2 < nst:
                                emitS(j + 2)
                            emitEPV(j)
                        pv3 = pv[:, 0:264].rearrange("p (i e) -> p i e", e=66)
                        K.op(DVE, lambda: nc.vector.reciprocal(out=rec4[:, hl * 4:hl * 4 + 4], in_=pv3[:, :, 64]),
                             r=[r_pv], w=[r_misc])
                        for il in range(4):
                            K.ts(DVE, attok[:, il, hl * 64:(hl + 1) * 64], pv3[:, il, 0:64],
                                 rec4[:, hl * 4 + il:hl * 4 + il + 1], None, ALU.mult, r=[r_pv, r_misc], w=[r_attok])
                    pst, r_pst = PS.next()
                    pstb = pst[:, 0:256].bitcast(BF16)
                    for il in range(4):
                        K.tr(pstb[:, il * 128:(il + 1) * 128], attok[:, il, :], identb, r=[r_attok, r_cbf], w=[r_pst],
                             signal=(il == 3))
                    K.copy(ACT, attT[:, hp, qc * 512:(qc + 1) * 512], pstb, r=[r_pst], w=[r_attT[qc]])
            if DEBUG_STAGE == "B" and b == 0:
                dump("attT", attT.rearrange("p c t -> p (c t)"), [P, 4 * S], BF16, r_attT)
                break

            K.mark("C%d" % b)
            K.barrier()
            r_uT, r_qT, r_kT, r_khat, r_vext, r_Sb, r_Dg = R("uT"), R("qT"), R("kT"), R("khat"), R("vext"), R("Sb"), R("Dg")
            r_Sf = [R("Sf0"), R("Sf1")]
            r_mu = [R("mu0"), R("mu1")]
            r_cc = R("cconst")
            r_cw = [R("cw%d" % i) for i in range(6)]
            r_cwb = [R("cwb%d" % i) for i in range(2)]
            r_sdtb = [R("sdtb%d" % i) for i in range(2)]
            K.dma(SP, bmvo, btok_d[:, 512:1536], w=[r_cc])
            K.dma(SP, mhg, mhg_d[:, :], w=[r_cc])
            for i in range(2):
                K.memset(DVE, muT[i][:, 0:4], 0.0, w=[r_mu[i]])
            K.memset(DVE, vext[:, :, 256:257], 1.0, w=[r_vext])
            K.memset(DVE, Sb[:, 0, :], 0.0, w=[r_Sb])
            cwi = [0]
            for h in range(4):
                K.mark('c_mu%d_%d' % (b, h))
                for fcl in range(2):
                    fc = 2 * h + fcl
                    for jj in range(4):
                        K.ts(DVE, Dg[:, fcl, jj, :], ident, convw[:, jj * 8 + fc:jj * 8 + fc + 1], None, ALU.mult,
                             r=[r_consts, r_misc], w=[r_Dg])
                for fcl in range(2):
                    fc = 2 * h + fcl
                    wm, rwm = WPF.get()
                    mu, rmu = muT[fcl], r_mu[fcl]
                    for tc in range(NTC):
                        ps, rp = PS.next()
                        for kc in range(8):
                            K.mm(ps[:, :], wm[:, kc * 128:(kc + 1) * 128], hT[:, kc, tc * 512:(tc + 1) * 512],
                                 start=(kc == 0), stop=(kc == 7), r=[rwm, r_hT[tc]], w=[rp])
                        K.actf(mu[:, 3 + tc * 512:3 + (tc + 1) * 512], ps[:, :], AF.Identity, bias=bfm[:, 8 + fc:9 + fc],
                               scale=1.0, r=[rp, r_misc], w=[rmu])
                    for tc in range(NTC):
                        ps, rp = PS.next()
                        for jj in range(4):
                            K.mm(ps[:, :], Dg[:, fcl, jj, :], mu[:, tc * 512 + jj:tc * 512 + jj + 512],
                                 start=(jj == 0), stop=(jj == 3), r=[r_Dg, rmu], w=[rp])
                        K.actf(uT[:, fcl, tc * 512:(tc + 1) * 512], ps[:, :], AF.Silu, bias=convb[:, fc:fc + 1], scale=1.0,
                               r=[rp, r_misc], w=[r_uT])
                K.mark('c_qk%d_%d' % (b, h))
                WPF.top_up()
                wqk, rwqk = WPF.get()
                for tc in range(NTC):
                    ps, rp = PS.next()
                    for ic in range(2):
                        K.mm(ps[:, :], wqk[:, ic * 128:(ic + 1) * 128], uT[:, ic, tc * 512:(tc + 1) * 512],
                             start=(ic == 0), stop=(ic == 1), r=[rwqk, r_uT], w=[rp])
                    K.op(ACT, lambda ps=ps, tc=tc: nc.scalar.mul(out=qT[:, tc * 512:(tc + 1) * 512], in_=ps[:, :], mul=128.0 ** -0.5),
                         r=[rp], w=[r_qT])
                    ps, rp = PS.next()
                    for ic in range(2):
                        K.mm(ps[:, :], wqk[:, 256 + ic * 128:256 + (ic + 1) * 128], uT[:, ic, tc * 512:(tc + 1) * 512],
                             start=(ic == 0), stop=(ic == 1), r=[rwqk, r_uT], w=[rp])
                    K.copy(DVE, kT[:, tc * 512:(tc + 1) * 512], ps[:, :], r=[rp], w=[r_kT])
                for j in range(NBLK):
                    ps, rp = PS.next()
                    jl = 0
                    for ic in range(2):
                        K.mm(ps[:, jl * 128:(jl + 1) * 128], uT[:, ic, j * 128:(j + 1) * 128],
                             wqk[:, 256 + ic * 128:256 + (ic + 1) * 128],
                             start=(ic == 0), stop=(ic == 1), r=[rwqk, r_uT], w=[rp])
                    K.ts(DVE, khat[:, j, :], ps[:, jl * 128:(jl + 1) * 128], ahat[:, j, h:h + 1], None, ALU.mult,
                         r=[rp, r_gates], w=[r_khat])
                K.mark('c_vext%d_%d' % (b, h))
                WPF.top_up()
                wmv, rwmv = WPF.get()
                for j in range(NBLK):
                    ps, rp = PS.next()
                    jl = 0
                    for kc in range(8):
                        K.mm(ps[:, jl * 256:(jl + 1) * 256], hT[:, kc, j * 128:(j + 1) * 128], wmv[:, kc * 256:(kc + 1) * 256],
                             start=(kc == 0), stop=(kc == 7), r=[rwmv, r_hT[j // 4]], w=[rp])
                    K.tt(DVE, vext[:, j, 0:256], ps[:, jl * 256:(jl + 1) * 256], bmvo[:, h * 256:(h + 1) * 256], ALU.add,
                         r=[rp, r_cc], w=[r_vext])
                K.mark('c_states%d_%d' % (b, h))
                WPF.top_up()
                K.memset(DVE, Sf[0], 0.0, w=[r_Sf[0]])
                for j in range(NBLK - 1):
                    ps, rp = PS.next()
                    K.mm(ps[:, 0:257], khat[:, j, :], vext[:, j, 0:257], True, True, r=[r_khat, r_vext], w=[rp])
                    cur, nxt = j % 2, (j + 1) % 2
                    K.stt(Sf[nxt], Sf[cur], EGt[:, j, h:h + 1], ps[:, 0:257], ALU.mult, ALU.add,
                          r=[r_Sf[cur], rp, r_gates], w=[r_Sf[nxt]])
                    K.copy(ACT, Sb[:, j + 1, 0:257], Sf[nxt], r=[r_Sf[nxt]], w=[r_Sb])
                K.mark('c_sig%d_%d' % (b, h))
                wmo, rwmo = WPF.get()
                sgT = uT
                for fcl in range(2):
                    fc = 2 * h + fcl
                    for tc in range(NTC):
                        ps, rp = PS.next()
                        for kc in range(8):
                            K.mm(ps[:, :], wmo[:, kc * 256 + fcl * 128:kc * 256 + (fcl + 1) * 128],
                                 hT[:, kc, tc * 512:(tc + 1) * 512],
                                 start=(kc == 0), stop=(kc == 7), r=[rwmo, r_hT[tc]], w=[rp])
                        K.actf(sgT[:, fcl, tc * 512:(tc + 1) * 512], ps[:, :], AF.Sigmoid, bias=bfm[:, 32 + fc:33 + fc], scale=1.0,
                               r=[rp, r_misc], w=[r_uT])
                        K.ts(DVE, sgT[:, fcl, tc * 512:(tc + 1) * 512], sgT[:, fcl, tc * 512:(tc + 1) * 512],
                             mhg[:, fc:fc + 1], None, ALU.mult, r=[r_uT, r_cc], w=[r_uT])
                WPF.top_up()
                K.mark('c_loop%d_%d' % (b, h))
                st_c = {}

                def c_s1(j):
                    pss, rps = PSC.next()
                    K.mm(pss[:, 0:128], kT[:, j * 128:(j + 1) * 128], qT[:, j * 128:(j + 1) * 128], True, True,
                         r=[r_kT, r_qT], w=[rps])
                    sdt, r_sdt = sdtb[j % 2], r_sdtb[j % 2]
                    K.stt(sdt, pss[:, 0:128], colfac[:, j, h:h + 1], mask01, ALU.mult, ALU.mult,
                          r=[rps, r_gates, r_consts], w=[r_sdt])
                    po, rpo = PSO.next()
                    K.mm(po[:, 0:257], qT[:, j * 128:(j + 1) * 128], Sb[:, j, 0:257], True, False, r=[r_qT, r_Sb], w=[rpo],
                         signal=False)
                    K.mm(po[:, 0:257], sdt, vext[:, j, 0:257], False, True, r=[r_sdt, r_vext], w=[rpo])
                    st_c[j] = (po, rpo)

                def c_s2(j):
                    po, rpo = st_c[j]
                    s, rs = smr()
                    K.ts(DVE, s[:, 4:5], po[:, 256:257], po[:, 256:257], enegb2[:, j, h:h + 1], ALU.mult, ALU.max,
                         r=[rpo, r_gates], w=[rs])
                    K.op(DVE, lambda: nc.vector.bn_stats(out=s[:, 6:12], in_=po[:, 0:256]), r=[rpo], w=[rs])
                    K.op(DVE, lambda: nc.vector.bn_aggr(out=s[:, 12:14], in_=s[:, 6:12]), r=[rs], w=[rs])
                    st_c[j] = (po, rpo, s, rs)

                def c_s2a(j):
                    po, rpo, s, rs = st_c[j]
                    rstd_act(s, rs, s[:, 13:14], in_ap=s[:, 4:5], scale=EPS)

                def c_s3(j):
                    po, rpo, s, rs = st_c[j]
                    hg, r_hg = cwb[j % 2], r_cwb[j % 2]
                    K.actf(hg, po[:, 0:256], AF.Identity, bias=s[:, 14:15], scale=s[:, 15:16], r=[rpo, rs], w=[r_hg])
                    st_c[j] = (hg, r_hg)

                def c_s3b(j):
                    hg, r_hg = st_c[j]
                    pst, r_pst = PSC.next()
                    pstb = pst[:, 0:128].bitcast(BF16)
                    for fcl in range(2):
                        K.tr(pstb[:, fcl * 128:(fcl + 1) * 128], hg[:, fcl * 128:(fcl + 1) * 128], identb,
                             r=[r_hg, r_cbf], w=[r_pst], signal=(fcl == 1))
                    st_c[j] = (pstb, r_pst)

                def c_s4(j):
                    pstb, r_pst = st_c.pop(j)
                    K.tt(DVE, hmT[:, 2 * h:2 * h + 2, j * 128:(j + 1) * 128], pstb.rearrange("p (c t) -> p c t", c=2),
                         sgT[:, :, j * 128:(j + 1) * 128], ALU.mult, r=[r_pst, r_uT], w=[r_hmT[j // 4]])

                okc = lambda x: 0 <= x < NBLK
                for t in range(NBLK + 3):
                    if okc(t - 1):
                        c_s2(t - 1)
                    if okc(t - 3):
                        c_s4(t - 3)
                    if okc(t - 2):
                        c_s3(t - 2)
                    if okc(t - 1):
                        c_s2a(t - 1)
                    if okc(t):
                        c_s1(t)
                    if okc(t - 2):
                        c_s3b(t - 2)
            if DEBUG_STAGE == "C" and b == 0:
                dump("hmT", hmT.rearrange("p c t -> p (c t)"), [P, 8 * S], BF16, r_hmT)
                break

            K.mark("D1%d" % b)
            K.barrier()
            r_d1 = [R("d1t%d" % i) for i in range(8)]
            r_wo = R("wo")
            for i in range(4):
                wload(wout_v[:, i * 2048:(i + 1) * 2048], wout_d[:, i * 2048:(i + 1) * 2048], [r_wo])
            d1i = [0]
            for dm in range(8):
                wa, rwa = WPF.get()
                wb_, rwb = WPF.get()
                WPF.top_up(2)
                for tc in range(NTC):
                    sl = slice(tc * 512, (tc + 1) * 512)
                    psga, rpga = PS.next()
                    for kc in range(8):
                        K.mm(psga[:, :], wb_[:, kc * 128:(kc + 1) * 128], hT[:, kc, sl], start=(kc == 0), stop=(kc == 7),
                             r=[rwb, r_hT[tc]], w=[rpga])
                    psgb, rpgb = PS.next()
                    for kc in range(8):
                        K.mm(psgb[:, :], wb_[:, 1024 + kc * 128:1024 + (kc + 1) * 128], hT[:, kc, sl], start=(kc == 0),
                             stop=(kc == 7), r=[rwb, r_hT[tc]], w=[rpgb])
                    psa, rpa = PS.next()
                    for ic in range(4):
                        K.mm(psa[:, :], wa[:, ic * 128:(ic + 1) * 128], attT[:, ic, sl], start=(ic == 0), stop=(ic == 3),
                             r=[rwa, r_attT[tc]], w=[rpa])
                    psb, rpb = PS.next()
                    for ic in range(8):
                        K.mm(psb[:, :], wa[:, 512 + ic * 128:512 + (ic + 1) * 128], hmT[:, ic, sl], start=(ic == 0),
                             stop=(ic == 7), r=[rwa, r_hmT[tc]], w=[rpb])
                    i0 = (d1i[0] % 2) * 4
                    d1i[0] += 1
                    sga, sgb, t1, t2 = d1t[i0], d1t[i0 + 1], d1t[i0 + 2], d1t[i0 + 3]
                    K.actf(sga, psga[:, :], AF.Sigmoid, bias=bfm[:, 16 + dm:17 + dm], scale=1.0, r=[rpga, r_misc], w=[r_d1[i0]])
                    K.actf(sgb, psgb[:, :], AF.Sigmoid, bias=bfm[:, 24 + dm:25 + dm], scale=1.0, r=[rpgb, r_misc], w=[r_d1[i0 + 1]])
                    K.tt(DVE, t1, psa[:, :], sga, ALU.mult, r=[rpa, r_d1[i0]], w=[r_d1[i0 + 2]])
                    K.tt(DVE, t2, psb[:, :], sgb, ALU.mult, r=[rpb, r_d1[i0 + 1]], w=[r_d1[i0 + 3]])
                    K.tt(POOL, yT[:, dm, sl], t1, t2, ALU.add, r=[r_d1[i0 + 2], r_d1[i0 + 3]], w=[r_yT[tc]])
                WPF.top_up()
            if DEBUG_STAGE == "D1" and b == 0:
                dump("yT", yT.rearrange("p c t -> p (c t)"), [P, 8 * S], BF16, r_yT)
                break

            K.mark("D2%d" % b)
            K.barrier()
            r_yog, r_ln = R("yog"), R("lnrep")
            h2T = hT
            K.dma(SP, lnrep[:, 0, :], lng_d[0], w=[r_ln])
            K.dma(SP, lnrep[:, 1, :], lng_d[1], w=[r_ln])
            r_win = [R("win0"), R("win1")]
            r_wdn = [R("wdn0"), R("wdn1")]
            r_actb = R("actb")
            r_sg = [R("sg0"), R("sg1")]
            sgt = [xns[0][:, 0:512], xns[1][:, 0:512]]
            sgi = [0]

            def ffn_load(g, extra=()):
                n = FG_SIZES[g]
                wi = winb[(g + 1) % 2]
                for kc in range(8):
                    wload(wi[:, kc, 0:2 * n * 128], wfin_d[g][:, kc * 2 * n * 128:(kc + 1) * 2 * n * 128],
                          [r_win[(g + 1) % 2]] + list(extra))
                wd = wdnb[g % 2]
                for f in range(n):
                    wload(wd[:, f, :], wfdn_d[g][:, f * D:(f + 1) * D], [r_wdn[g % 2]] + list(extra))

            st_d = {}

            def d_wout(tc):
                sl = slice(tc * 512, (tc + 1) * 512)
                for dm in range(8):
                    ps, rp = PS.next()
                    for ic in range(8):
                        K.mm(ps[:, :], wout_v[:, ic * 1024 + dm * 128:ic * 1024 + (dm + 1) * 128], yT[:, ic, sl],
                             start=(ic == 0), stop=(ic == 7), r=[r_wo, r_yT[tc]], w=[rp])
                    K.actf(yog[:, dm, :], ps[:, :], AF.Identity, scale=g1[:, dm, b:b + 1], r=[rp, r_mod], w=[r_yog])

            def d_a(blk):
                tc, tl = blk // 4, blk % 4
                if tl == 0:
                    d_wout(tc)
                xt, r_xt = xld2.pop(blk)
                for half in range(2):
                    ps, rp = PS.next()
                    for f in range(4):
                        dm = half * 4 + f
                        K.tr(ps[:, f * 128:(f + 1) * 128], yog[:, dm, tl * 128:(tl + 1) * 128], ident,
                             r=[r_yog, r_consts], w=[rp], signal=(f == 3))
                    K.stt(xt[:, half * 512:(half + 1) * 512], xt[:, half * 512:(half + 1) * 512], ALPHA, ps[:, :],
                          ALU.mult, ALU.add, r=[r_xt, rp], w=[r_xt])
                sA, rsA = ln_stats(xt, r_xt, with_act=False)
                st_d[blk] = (xt, r_xt, sA, rsA)

            xld2 = {}

            def d_0(blk):
                xt, r_xt = XT.next()
                K.dma(SP, xt[:], x_d[b, blk * 128:(blk + 1) * 128, :], w=[r_xt])
                xld2[blk] = (xt, r_xt)

            def d_b(blk):
                xt, r_xt, sA, rsA = st_d[blk]
                rstd_act(sA, rsA, EPS)
                K.actf(xt[:], xt[:], AF.Identity, bias=sA[:, 14:15], scale=sA[:, 15:16], r=[r_xt, rsA], w=[r_xt])

            def d_c(blk):
                xt, r_xt, sA, rsA = st_d[blk]
                K.tt(DVE, xt[:], xt[:], lnrep[:, 0, :], ALU.mult, r=[r_xt, r_ln], w=[r_xt])
                K.tt(DVE, xt[:], xt[:], lnrep[:, 1, :], ALU.add, r=[r_xt, r_ln], w=[r_xt])

            def d_d(blk):
                tc = blk // 4
                xt, r_xt, sA, rsA = st_d[blk]
                for half in range(2):
                    ps, rp = PS.next()
                    for f in range(4):
                        dm = half * 4 + f
                        K.tr(ps[:, f * 128:(f + 1) * 128], xt[:, dm * 128:(dm + 1) * 128], ident,
                             r=[r_xt, r_consts], w=[rp], signal=(f == 3))
                    K.op(ACT, lambda ps=ps, half=half, blk=blk: nc.scalar.mul(
                        out=accT[:, half * 4:half * 4 + 4, blk * 128:(blk + 1) * 128],
                        in_=ps[:, :].rearrange("p (c t) -> p c t", c=4), mul=ALPHA), r=[rp], w=[r_acc[tc]])
                sB, rsB = ln_stats(xt, r_xt, with_act=False)
                st_d[blk] = (xt, r_xt, sB, rsB)

            def d_e(blk):
                xt, r_xt, sB, rsB = st_d[blk]
                rstd_act(sB, rsB, EPS)
                st_d[blk] = ln_norm(xt, r_xt, sB[:, 15:16], sB[:, 14:15], rsB)

            def d_f(blk):
                xn, r_xn = st_d.pop(blk)
                ln_T(b, blk, xn, r_xn, h2T, r_h2T[blk // 4], sc2p, sh2, all_act=True)

            ok = lambda t: 0 <= t < NBLK
            d_0(0)
            for i in range(NBLK + 2):
                if ok(i):
                    d_a(i)
                if i == 12:
                    ffn_load(0, extra=r_yT)
                if ok(i - 1):
                    d_b(i - 1)
                if ok(i - 2):
                    d_e(i - 2)
                if ok(i + 1):
                    d_0(i + 1)
                if ok(i - 1):
                    d_c(i - 1)
                    d_d(i - 1)
                if ok(i - 2):
                    d_f(i - 2)
            if DEBUG_STAGE == "D2" and b == 0:
                dump("accT", accT.rearrange("p c t -> p (c t)"), [P, 8 * S], F32, r_acc)
                dump("h2T", h2T.rearrange("p c t -> p (c t)"), [P, 8 * S], BF16, r_h2T)
                break

            K.mark("E%d" % b)
            K.barrier()
            for g, n in enumerate(FG_SIZES):
                if g + 1 < len(FG_SIZES):
                    ffn_load(g + 1)
                wi, wd = winb[(g + 1) % 2], wdnb[g % 2]
                rwi, rwd = r_win[(g + 1) % 2], r_wdn[g % 2]
                for f in range(n):
                    for tc in range(NTC):
                        sl = slice(tc * 512, (tc + 1) * 512)
                        psg_, rpg = PS.next()
                        for kc in range(8):
                            K.mm(psg_[:, :], wi[:, kc, f * 128:(f + 1) * 128], h2T[:, kc, sl], start=(kc == 0), stop=(kc == 7),
                                 r=[rwi, r_h2T[tc]], w=[rpg])
                        psu, rpu = PS.next()
                        for kc in range(8):
                            K.mm(psu[:, :], wi[:, kc, n * 128 + f * 128:n * 128 + (f + 1) * 128], h2T[:, kc, sl], start=(kc == 0),
                                 stop=(kc == 7), r=[rwi, r_h2T[tc]], w=[rpu])
                        si = sgi[0] % 2
                        sgi[0] += 1
                        K.actf(sgt[si], psg_[:, :], AF.Silu, r=[rpg], w=[r_sg[si]])
                        K.tt(DVE, actb[:, f, sl], psu[:, :], sgt[si], ALU.mult, r=[rpu, r_sg[si]], w=[r_actb])
                for tc in range(NTC):
                    sl = slice(tc * 512, (tc + 1) * 512)
                    for dm in range(8):
                        ps, rp = PS.next()
                        for f in range(n):
                            K.mm(ps[:, :], wd[:, f, dm * 128:(dm + 1) * 128], actb[:, f, sl], start=(f == 0), stop=(f == n - 1),
                                 r=[rwd, r_actb], w=[rp])
                        K.stt(accT[:, dm, sl], ps[:, :], g2[:, dm, b:b + 1], accT[:, dm, sl], ALU.mult, ALU.add,
                              r=[rp, r_mod, r_acc[tc]], w=[r_acc[tc]])

            K.mark("F%d" % b)
            K.barrier()
            K.dma(SP, lnrep[:, 0, :], lng_d[2], w=[r_ln])
            K.dma(SP, lnrep[:, 1, :], lng_d[3], w=[r_ln])
            st_f = {}

            def f_s1(blk):
                xt, r_xt = XT.next()
                for half in range(2):
                    ps, rp = PS.next()
                    for f in range(4):
                        dm = half * 4 + f
                        K.tr(ps[:, f * 128:(f + 1) * 128], accT[:, dm, blk * 128:(blk + 1) * 128], ident,
                             r=[r_acc[blk // 4], r_consts], w=[rp], signal=(f == 3))
                    K.copy(ACT, xt[:, half * 512:(half + 1) * 512], ps[:, :], r=[rp], w=[r_xt])
                sF, rsF = ln_stats(xt, r_xt, with_act=False)
                st_f[blk] = (xt, r_xt, sF, None, rsF)

            def f_s2a(blk):
                xt, r_xt, rstd, nmr, rs = st_f[blk]
                rstd_act(rstd, rs, EPS)
                K.actf(xt[:], xt[:], AF.Identity, bias=rstd[:, 14:15], scale=rstd[:, 15:16], r=[r_xt, rs], w=[r_xt])

            def f_s2b(blk):
                xt, r_xt, rstd, nmr, rs = st_f.pop(blk)
                K.tt(DVE, xt[:], xt[:], lnrep[:, 0, :], ALU.mult, r=[r_xt, r_ln], w=[r_xt])
                K.tt(DVE, xt[:], xt[:], lnrep[:, 1, :], ALU.add, r=[r_xt, r_ln], w=[r_xt])
                K.dma(SP, out_d[b, blk * 128:(blk + 1) * 128, :], xt[:], r=[r_xt], is_out=True)

            for t in range(NBLK + 1):
                if t < NBLK:
                    f_s1(t)
                if 0 <= t - 1 < NBLK:
                    f_s2a(t - 1)
                    f_s2b(t - 1)
            K.barrier()

        for nm, v in K.out_events.items():
            if SP.known.get(nm, 0) < v:
                nc.sync.wait_ge(K.sems[nm], v)
                SP.known[nm] = v
        K.mark("end")
        build_program.n_ins = K.n_ins
        build_program.marks = K.marks
    return nc


def _grp(w, col0, ncols, gw):
    K_ = w.shape[0]
    sub = w[:, col0:col0 + ncols].reshape(K_ // 128, 128, ncols // gw, gw)
    return np.ascontiguousarray(sub.transpose(2, 1, 0, 3).reshape(ncols // gw, 128, (K_ // 128) * gw))


def _colT(v):
    return np.ascontiguousarray(v.reshape(-1, 128).T)


def _rep(v):
    return np.ascontiguousarray(np.broadcast_to(v[None, :], (128, v.shape[0])))


def prepare_shared(inp):
    f = lambda k: np.asarray(inp[k], dtype=np.float32)[0]
    w_in, b_in = f("w_in"), f("b_in")
    sh = {}
    sh["wada"] = _grp(f("w_ada"), 0, 6144, 128)
    sh["badaT"] = _colT(f("b_ada"))
    sh["w128"] = np.concatenate([_grp(w_in, 0, 512, 128), _grp(w_in, 512, 512, 128), _grp(w_in, 1024, 512, 128),
                                 _grp(w_in, 1544, 1024, 128)], axis=0)
    sh["w256"] = np.concatenate([_grp(w_in, 2568, 1024, 256), _grp(w_in, 3600, 1024, 256)], axis=0)
    gcols = np.concatenate([w_in[:, 1536:1544], w_in[:, 3592:3600]], axis=1)
    sh["wgate"] = _grp(gcols, 0, 16, 16)[0]
    wq, wk = f("w_mq"), f("w_mk")
    sh["wqk"] = np.stack([np.concatenate([_grp(wq[h], 0, 128, 128)[0], _grp(wk[h], 0, 128, 128)[0]], axis=1)
                          for h in range(4)], axis=0)
    wpa, wpb = f("w_pa"), f("w_pb")
    sh["wd1a"] = np.concatenate([_grp(wpa, 0, 1024, 128), _grp(wpb, 0, 1024, 128)], axis=2)
    sh["wd1b"] = np.concatenate([_grp(w_in, 4624, 1024, 128), _grp(w_in, 5648, 1024, 128)], axis=2)
    sh["wout"] = _grp(f("w_out"), 0, 1024, 1024)[0]
    wfi, wfd = f("w_ffn_in"), f("w_ffn_down")
    c0 = 0
    for g, n in enumerate(FG_SIZES):
        gate = wfi[:, c0 * 128:(c0 + n) * 128].reshape(8, 128, n * 128)
        up = wfi[:, DFF + c0 * 128:DFF + (c0 + n) * 128].reshape(8, 128, n * 128)
        both = np.concatenate([gate, up], axis=2)
        sh["wfin%d" % g] = np.ascontiguousarray(both.transpose(1, 0, 2).reshape(128, 8 * 2 * n * 128))
        dn = wfd[c0 * 128:(c0 + n) * 128, :].reshape(n, 128, D)
        sh["wfdn%d" % g] = np.ascontiguousarray(dn.transpose(1, 0, 2).reshape(128, n * D))
        c0 += n
    sh["bfm"] = np.concatenate([_colT(b_in[0:512]), _colT(b_in[512:1024]), _colT(b_in[1544:2568]),
                                _colT(b_in[4624:5648]), _colT(b_in[5648:6672]), _colT(b_in[3600:4624])], axis=1)
    bg = np.concatenate([b_in[1536:1544], b_in[3592:3600]])
    sh["bgate"] = _rep(np.tile(bg, 16))
    sh["btok"] = _rep(np.concatenate([b_in[1024:1536], b_in[2568:3592], b_in[3600:4624]]))
    sh["convw"] = np.ascontiguousarray(f("conv_w").reshape(4, 8, 128).transpose(2, 0, 1).reshape(128, 32))
    sh["convb"] = _colT(f("conv_b"))
    sh["mhg"] = _colT(f("mh_norm_g"))
    sh["lng"] = np.stack([_rep(f("ln1_g")), _rep(f("ln1_b")), _rep(f("ln2_g")), _rep(f("ln2_b"))], axis=0)
    eye = np.eye(128, dtype=np.float32)
    tri = np.triu(np.ones((128, 128), np.float32))
    ones = np.ones((128, 128), np.float32)
    mneg = np.where(tri > 0, 0.0, -30000.0).astype(np.float32)
    sh["consts"] = np.ascontiguousarray(np.concatenate([eye, tri, ones, mneg, tri], axis=1))
    return {k: np.ascontiguousarray(v, dtype=np.float32) for k, v in sh.items()}


def core_inputs(inp, shared, core):
    x = np.asarray(inp["x"], dtype=np.float32)
    c = np.asarray(inp["c"], dtype=np.float32)
    m = dict(shared)
    m["x"] = np.ascontiguousarray(x[core * NB:(core + 1) * NB])
    cl = c[core * NB:(core + 1) * NB]
    m["cT"] = np.ascontiguousarray(cl.reshape(NB, 8, 128).transpose(2, 1, 0).reshape(128, 16))
    return m


def kernel(**inputs):
    n = 8
    shared = prepare_shared(inputs)
    nc = build_program()
    in_maps = [core_inputs(inputs, shared, i) for i in range(n)]
    res = run_bass_kernel_spmd(nc, in_maps, core_ids=list(range(n)))
    out = np.concatenate([np.asarray(r["out"]) for r in res.results], axis=0)
    return out.astype(np.float32, copy=False)
```
